# Optimizing a Trainium2 kernel written in Bass

```python
import jax, jax.numpy as jnp
from jax import lax
import numpy as np

D_MODEL = 1024
BATCH = 8
SEQ = 4096
DEPTH = 4

N_MIXERS = 4
D_FF = ((8 * D_MODEL // 3 + 127) // 128) * 128
PLE_DIM = 256
CHUNK = 128
A_HALF = 3 * D_MODEL
A_GROUPS = A_HALF // 256
POOL_WINDOWS = (2, 4, 8, 16)
POOL_GROUP = D_MODEL // len(POOL_WINDOWS)
CONV_WIDTH = 31
SHORT_CONV_WIDTH = 3
EPS = 1e-6

kernel_name = "hybrid_interleaved_gmlp_pool_conformer_shortconv"


def n_layers_of(m):
    return len(range(m, DEPTH, N_MIXERS))


def rms_norm(x, g):
    xf = x.astype(jnp.float32)
    y = xf * lax.rsqrt(jnp.mean(xf * xf, axis=-1, keepdims=True) + EPS)
    return (y * g.astype(jnp.float32)).astype(x.dtype)


def layer_norm(x, g, b):
    xf = x.astype(jnp.float32)
    mu = jnp.mean(xf, axis=-1, keepdims=True)
    xc = xf - mu
    y = xc * lax.rsqrt(jnp.mean(xc * xc, axis=-1, keepdims=True) + EPS)
    return (y * g.astype(jnp.float32) + b.astype(jnp.float32)).astype(x.dtype)


def swiglu(x, w_gate, w_up, w_down):
    return (jax.nn.silu(x @ w_gate) * (x @ w_up)) @ w_down


def causal_depthwise_conv(x, w, b=None):
    k, c = w.shape
    y = lax.conv_general_dilated(
        x, w[:, None, :].astype(x.dtype), window_strides=(1,),
        padding=((k - 1, 0),), dimension_numbers=('NWC', 'WIO', 'NWC'),
        feature_group_count=c)
    return y if b is None else y + b


def mixer_gmlp_chunked(h, w_in, v_g, v_b, w_s, b_s, w_out):
    B, S, _ = h.shape
    z = jax.nn.gelu(h @ w_in)
    u, v = jnp.split(z, 2, axis=-1)
    v = layer_norm(v, v_g, v_b)
    n_chunks = S // CHUNK
    v = v.reshape(B, n_chunks, CHUNK, A_GROUPS, A_HALF // A_GROUPS)
    mask = jnp.tril(jnp.ones((CHUNK, CHUNK), dtype=bool))
    ws = jnp.where(mask[None], w_s, jnp.zeros((), w_s.dtype)).astype(v.dtype)
    sv = jnp.einsum('gts,bnsgc->bntgc', ws, v) + b_s.T[None, None, :, :, None]
    y = u * sv.reshape(B, S, A_HALF)
    return y @ w_out


def mixer_multiscale_pool(h, w_grp, scale):
    B, S, D = h.shape
    hf = h.astype(jnp.float32)
    cs = jnp.cumsum(hf, axis=1)
    t = jnp.arange(1, S + 1, dtype=jnp.float32)[None, :, None]
    outs = []
    for g, w in enumerate(POOL_WINDOWS):
        sl = slice(g * POOL_GROUP, (g + 1) * POOL_GROUP)
        c = cs[..., sl]
        prev = jnp.pad(c, ((0, 0), (w, 0), (0, 0)))[:, :S]
        mean = (c - prev) / jnp.minimum(t, w)
        outs.append(mean - hf[..., sl])
    pooled = jnp.stack(outs, axis=2).astype(h.dtype)
    y = jnp.einsum('bsgc,gcd->bsgd', pooled, w_grp).reshape(B, S, D)
    return y * scale


def mixer_conformer_conv(h, w_pw1, w_dw, b_dw, n_g, n_b, w_pw2):
    a, g = jnp.split(h @ w_pw1, 2, axis=-1)
    z = a * jax.nn.sigmoid(g)
    z = causal_depthwise_conv(z, w_dw, b_dw)
    z = jax.nn.silu(layer_norm(z, n_g, n_b))
    return z @ w_pw2


def mixer_short_gated_conv(h, w_in, w_conv, w_out):
    bg, cg, xv = jnp.split(h @ w_in, 3, axis=-1)
    return (bg * causal_depthwise_conv(cg * xv, w_conv)) @ w_out


def setup_inputs(seed: int = 0) -> dict:
    key = jax.random.key(seed)
    keys = iter(jax.random.split(key, 64))
    f32 = jnp.float32

    def nrm(shape, scale):
        return jax.random.normal(next(keys), shape, f32) * scale

    def dense(shape, fan_in):
        return nrm(shape, fan_in ** -0.5)

    def gain(shape):
        return 1.0 + nrm(shape, 0.05)

    L, D, F = DEPTH, D_MODEL, D_FF
    nA, nB, nC, nD = n_layers_of(0), n_layers_of(1), n_layers_of(2), n_layers_of(3)
    Gc = A_HALF // A_GROUPS
    return {
        "x": nrm((BATCH, SEQ, D), 1.0),
        "p": nrm((DEPTH, BATCH, SEQ, PLE_DIM), 1.0),
        "ff1_pre_g": gain((L, D)),
        "ff1_w_gate": dense((L, D, F), D),
        "ff1_w_up": dense((L, D, F), D),
        "ff1_w_down": dense((L, F, D), F),
        "ff1_post_g": gain((L, D)),
        "mix_pre_g": gain((L, D)),
        "mix_post_g": gain((L, D)),
        "ff2_pre_g": gain((L, D)),
        "ff2_w_gate": dense((L, D, F), D),
        "ff2_w_up": dense((L, D, F), D),
        "ff2_w_down": dense((L, F, D), F),
        "ff2_post_g": gain((L, D)),
        "ple_gate_norm_g": gain((L, D)),
        "ple_w_gate": dense((L, D, D), D),
        "ple_w_proj": dense((L, PLE_DIM, D), PLE_DIM),
        "ple_post_g": gain((L, D)),
        "a_w_in": dense((nA, D, 2 * A_HALF), D),
        "a_v_norm_g": gain((nA, A_HALF)),
        "a_v_norm_b": nrm((nA, A_HALF), 0.02),
        "a_w_s": dense((nA, A_GROUPS, CHUNK, CHUNK), CHUNK),
        "a_b_s": 1.0 + nrm((nA, A_GROUPS, CHUNK), 0.05),
        "a_w_out": dense((nA, A_HALF, D), A_HALF),
        "b_w_grp": dense((nB, len(POOL_WINDOWS), POOL_GROUP, POOL_GROUP), POOL_GROUP),
        "b_scale": gain((nB, D)) + nrm((nB, D), 0.05),
        "c_w_pw1": dense((nC, D, 2 * D), D),
        "c_w_dw": dense((nC, CONV_WIDTH, D), CONV_WIDTH),
        "c_b_dw": nrm((nC, D), 0.02),
        "c_norm_g": gain((nC, D)),
        "c_norm_b": nrm((nC, D), 0.02),
        "c_w_pw2": dense((nC, D, D), D),
        "d_w_in": dense((nD, D, 3 * D), D),
        "d_w_conv": dense((nD, SHORT_CONV_WIDTH, D), SHORT_CONV_WIDTH),
        "d_w_out": dense((nD, D, D), D),
    }


def reference(x, p,
              ff1_pre_g, ff1_w_gate, ff1_w_up, ff1_w_down, ff1_post_g,
              mix_pre_g, mix_post_g,
              ff2_pre_g, ff2_w_gate, ff2_w_up, ff2_w_down, ff2_post_g,
              ple_gate_norm_g, ple_w_gate, ple_w_proj, ple_post_g,
              a_w_in, a_v_norm_g, a_v_norm_b, a_w_s, a_b_s, a_w_out,
              b_w_grp, b_scale,
              c_w_pw1, c_w_dw, c_b_dw, c_norm_g, c_norm_b, c_w_pw2,
              d_w_in, d_w_conv, d_w_out):
    h = x
    for i in range(DEPTH):
        f = swiglu(rms_norm(h, ff1_pre_g[i]), ff1_w_gate[i], ff1_w_up[i], ff1_w_down[i])
        h = h + 0.5 * rms_norm(f, ff1_post_g[i])
        hn = rms_norm(h, mix_pre_g[i])
        m, j = i % N_MIXERS, i // N_MIXERS
        if m == 0:
            y = mixer_gmlp_chunked(hn, a_w_in[j], a_v_norm_g[j], a_v_norm_b[j],
                                   a_w_s[j], a_b_s[j], a_w_out[j])
        elif m == 1:
            y = mixer_multiscale_pool(hn, b_w_grp[j], b_scale[j])
        elif m == 2:
            y = mixer_conformer_conv(hn, c_w_pw1[j], c_w_dw[j], c_b_dw[j],
                                     c_norm_g[j], c_norm_b[j], c_w_pw2[j])
        else:
            y = mixer_short_gated_conv(hn, d_w_in[j], d_w_conv[j], d_w_out[j])
        h = h + rms_norm(y, mix_post_g[i])
        f = swiglu(rms_norm(h, ff2_pre_g[i]), ff2_w_gate[i], ff2_w_up[i], ff2_w_down[i])
        h = h + 0.5 * rms_norm(f, ff2_post_g[i])
        gate = jax.nn.sigmoid(rms_norm(h, ple_gate_norm_g[i]) @ ple_w_gate[i])
        e = (p[i].astype(h.dtype) @ ple_w_proj[i]) * gate
        h = h + rms_norm(e, ple_post_g[i])
    return h
```

```python
import contextlib
import math
import numpy as np
import concourse.bass as bass
import concourse.mybir as mybir
from concourse.bass_utils import run_bass_kernel_spmd

F32 = mybir.dt.float32
BF16 = mybir.dt.bfloat16
I32 = mybir.dt.int32
AF = mybir.ActivationFunctionType
ALU = mybir.AluOpType

PE, ACT, DVE, POOL, SP = "tensor", "scalar", "vector", "gpsimd", "sync"
ENGINES = (PE, ACT, DVE, POOL, SP)


def I(name, *a, **kw):
    return lambda e: getattr(e, name)(*a, **kw)


class WSrc:
    def __init__(self, f32, bf, key):
        self.f32, self.bf, self.key = f32, bf, key

    def __getitem__(self, idx):
        return WSrc(self.f32[idx], self.bf[idx], self.key + (repr(idx),))

    def rearrange(self, pat, **kw):
        return WSrc(self.f32.rearrange(pat, **kw), self.bf.rearrange(pat, **kw), self.key + (pat,))


class Buf:
    __slots__ = ("name", "w", "r")

    def __init__(self, name):
        self.name = name
        self.w = None
        self.r = []


class Op:
    __slots__ = ("eng", "fn", "deps", "idx", "is_dma", "sem", "val", "needs_inc")

    def __init__(self, eng, fn, is_dma):
        self.eng = eng
        self.fn = fn
        self.deps = []
        self.is_dma = is_dma
        self.sem = None
        self.val = None
        self.needs_inc = is_dma


class Sched:
    def __init__(self, nc):
        self.nc = nc
        self.ops = {e: [] for e in ENGINES}
        self.dma_sems = []
        self.eng_sem = {}

    def _add_dep(self, o, d):
        if d is None or d is o:
            return
        if (not d.is_dma) and d.eng == PE and o.eng == PE and not o.is_dma:
            return
        o.deps.append(d)

    def op(self, eng, fn, reads=(), writes=(), dma_slot=None, relax=False):
        is_dma = dma_slot is not None
        o = Op(eng, fn, is_dma)
        o.idx = len(self.ops[eng])
        for b in reads:
            self._add_dep(o, b.w)
        for b in writes:
            self._add_dep(o, b.w)
            for r in b.r:
                self._add_dep(o, r)
        if relax:
            o.deps = [d for d in o.deps if d.is_dma or d.eng != eng]
        for b in reads:
            b.r.append(o)
        for b in writes:
            b.w = o
            b.r = []
        if is_dma:
            dma_slot[1] += 16
            o.sem = dma_slot
            o.val = dma_slot[1]
        self.ops[eng].append(o)
        return o

    def new_dma_slot(self):
        s = [None, 0]
        self.dma_sems.append(s)
        return s

    def finalize_and_emit(self, final_waits=()):
        nc = self.nc
        for e in ENGINES:
            for o in self.ops[e]:
                best = {}
                dmas = {}
                for d in o.deps:
                    if d.is_dma:
                        k = id(d.sem)
                        if k not in dmas or dmas[k].val < d.val:
                            dmas[k] = d
                    else:
                        k = d.eng
                        if k not in best or best[k].idx < d.idx:
                            best[k] = d
                o.deps = list(best.values()) + list(dmas.values())
                for d in best.values():
                    d.needs_inc = True
        for d in final_waits:
            d.needs_inc = True
        for e in ENGINES:
            c = 0
            for o in self.ops[e]:
                if not o.is_dma and o.needs_inc:
                    c += 1
                    o.val = c
        with contextlib.ExitStack() as st:
            for e in (PE, ACT, DVE, POOL):
                self.eng_sem[e] = st.enter_context(nc.semaphore("s_" + e))
            for i, s in enumerate(self.dma_sems):
                s[0] = st.enter_context(nc.semaphore("d%d" % i))
            block = st.enter_context(nc.Block())
            sched = self

            def emit(e, eng):
                waited = {}
                for o in sched.ops[e]:
                    for d in o.deps:
                        if d.is_dma:
                            sem, val, key = d.sem[0], d.val, id(d.sem)
                        else:
                            sem, val, key = sched.eng_sem[d.eng], d.val, d.eng
                        if waited.get(key, 0) < val:
                            eng.wait_ge(sem, val)
                            waited[key] = val
                    ins = o.fn(eng)
                    if o.is_dma:
                        ins.then_inc(o.sem[0], 16)
                    elif o.needs_inc:
                        ins.then_inc(sched.eng_sem[e], 1)
                if e == SP:
                    for d in final_waits:
                        if d.is_dma:
                            eng.wait_ge(d.sem[0], d.val)
                        else:
                            eng.wait_ge(sched.eng_sem[d.eng], d.val)

            @block.tensor
            def _(eng):
                emit(PE, eng)

            @block.scalar
            def _(eng):
                emit(ACT, eng)

            @block.vector
            def _(eng):
                emit(DVE, eng)

            @block.gpsimd
            def _(eng):
                emit(POOL, eng)

            @block.sync
            def _(eng):
                emit(SP, eng)


class Cfg:
    def __init__(self, D=1024, F=2816, S=4096, PLE=256, DEPTH=4, CW=31, SW=3, NSLOT=13):
        self.D, self.F, self.S, self.PLE, self.DEPTH, self.CW, self.SW = D, F, S, PLE, DEPTH, CW, SW
        self.AH = 3 * D
        self.AG = self.AH // 256
        self.PG = D // 4
        self.KD = D // 128
        self.KF = F // 128
        self.KA = self.AH // 128
        self.KP = PLE // 128
        self.TB = 512
        self.NT = 4
        self.NBLK = S // self.TB
        self.NSLOT = NSLOT
        self.WINS = (2, 4, 8, 16)
        self.EPS = 1e-6
        assert self.PG % 128 == 0 and F % 256 == 0 and self.KA % 4 == 0


FULL = Cfg()

WEIGHT_NAMES = [
    "ff1_pre_g", "ff1_w_gate", "ff1_w_up", "ff1_w_down", "ff1_post_g", "mix_pre_g", "mix_post_g",
    "ff2_pre_g", "ff2_w_gate", "ff2_w_up", "ff2_w_down", "ff2_post_g",
    "ple_gate_norm_g", "ple_w_gate", "ple_w_proj", "ple_post_g",
    "a_w_in", "a_v_norm_g", "a_v_norm_b", "a_w_s", "a_b_s", "a_w_out", "b_w_grp", "b_scale",
    "c_w_pw1", "c_w_dw", "c_b_dw", "c_norm_g", "c_norm_b", "c_w_pw2", "d_w_in", "d_w_conv", "d_w_out",
]


def weight_shapes(c):
    L, D, F = c.DEPTH, c.D, c.F
    nA = len(range(0, L, 4)); nB = len(range(1, L, 4)); nC = len(range(2, L, 4)); nD = len(range(3, L, 4))
    return {
        "ff1_pre_g": (L, D), "ff1_w_gate": (L, D, F), "ff1_w_up": (L, D, F), "ff1_w_down": (L, F, D),
        "ff1_post_g": (L, D), "mix_pre_g": (L, D), "mix_post_g": (L, D), "ff2_pre_g": (L, D),
        "ff2_w_gate": (L, D, F), "ff2_w_up": (L, D, F), "ff2_w_down": (L, F, D), "ff2_post_g": (L, D),
        "ple_gate_norm_g": (L, D), "ple_w_gate": (L, D, D), "ple_w_proj": (L, c.PLE, D), "ple_post_g": (L, D),
        "a_w_in": (nA, D, 2 * c.AH), "a_v_norm_g": (nA, c.AH), "a_v_norm_b": (nA, c.AH),
        "a_w_s": (nA, c.AG, 128, 128), "a_b_s": (nA, c.AG, 128), "a_w_out": (nA, c.AH, D),
        "b_w_grp": (nB, 4, c.PG, c.PG), "b_scale": (nB, D),
        "c_w_pw1": (nC, D, 2 * D), "c_w_dw": (nC, c.CW, D), "c_b_dw": (nC, D), "c_norm_g": (nC, D),
        "c_norm_b": (nC, D), "c_w_pw2": (nC, D, D),
        "d_w_in": (nD, D, 3 * D), "d_w_conv": (nD, c.SW, D), "d_w_out": (nD, D, D),
    }


class Builder:
    def __init__(self, c, layers):
        self.c = c
        self.layers = list(layers)
        nc = self.nc = bass.Bass("TRN2", target_bir_lowering=False)
        self.S = Sched(nc)
        D, KD = c.D, c.KD
        self.x = nc.dram_tensor("x", [c.S, D], F32, kind="ExternalInput").ap()
        self.p = nc.dram_tensor("p", [c.DEPTH, c.S, c.PLE], F32, kind="ExternalInput").ap()
        self.y = nc.dram_tensor("y", [c.S, D], F32, kind="ExternalOutput").ap()
        self.w = {}
        for n, shp in weight_shapes(c).items():
            self.w[n] = nc.dram_tensor(n, list(shp), F32, kind="ExternalInput").ap()
        A = nc.alloc_sbuf_tensor
        self.h = A("h", [128, c.NT, D], F32)
        self.hs = A("hs", [128, c.NT + 1, D], BF16)
        self.xnT = A("xnT", [128, KD, c.TB], BF16)
        self.NACT = max(c.KA, c.KF, KD)
        self.actT = A("actT", [128, self.NACT, c.TB], BF16)
        self.ring = A("ring", [128, c.NSLOT, 2048], BF16)
        self.gb = A("gb", [128, 4, D], F32)
        self.NV = 64
        self.vecT = A("vecT", [128, KD, self.NV], F32)
        self.ident = A("ident", [128, 128], BF16)
        self.identf = A("identf", [128, 128], F32)
        self.onesD = A("onesD", [128, 128], F32)
        self.ones_row = A("ones_row", [1, 128], BF16)
        self.sil = A("sil", [128, 2, c.TB], F32)
        self.tmp = A("tmp", [128, 2, D], F32)
        self.stat = A("stat", [128, 192], F32)
        self.MSB = 48 * 1024
        self.ms = A("ms", [128, self.MSB // 4], F32)
        self.halo_c = A("halo_c", [128, KD, c.CW - 1], BF16)
        self.gt = A("gt", [128, KD, 128], F32)
        self.eps_col = A("eps_col", [128, 8], F32)
        self.dg = None
        self.b_dg = Buf("dg")
        self.b_gt = Buf("gt")
        self.halo_d = A("halo_d", [128, KD, c.SW - 1], F32)
        self.wsT = A("wsT", [128, c.AG, 128], BF16)
        self.bs_row = A("bs_row", [1, c.AG * 128], BF16)
        self.PM = A("PM", [128, 4, 3, 128], BF16)
        self.fence_t = A("fence_t", [128, 8], F32)
        self.ps = nc.alloc_psum_tensor("ps", [128, 8, 512], F32)
        B = Buf
        self.b_h = [B("h%d" % i) for i in range(c.NT)]
        self.b_hs = [B("hs%d" % i) for i in range(c.NT + 1)]
        self.b_xnT = [B("xnT%d" % i) for i in range(c.NT)]
        self.b_act = [B("act%d" % i) for i in range(self.NACT)]
        self.b_ring = [B("ring%d" % i) for i in range(c.NSLOT)]
        self.d_ring = [self.S.new_dma_slot() for _ in range(c.NSLOT)]
        self.b_gb = [B("gb%d" % i) for i in range(4)]
        self.d_gb = [self.S.new_dma_slot() for _ in range(4)]
        self.b_ps = [B("ps%d" % i) for i in range(8)]
        self.b_sil = [B("sil0"), B("sil1")]
        self.b_tmp = [B("tmp0"), B("tmp1")]
        self.b_const = B("const")
        self.b_vecT = B("vecT")
        self.d_p = self.S.new_dma_slot()
        self.b_ms = {}
        self.d_x = [self.S.new_dma_slot() for _ in range(c.NT)]
        self.d_y = [self.S.new_dma_slot() for _ in range(c.NT)]
        self.b_halo_c = B("halo_c")
        self.b_halo_d = B("halo_d")
        self.b_wsT = B("wsT")
        self.b_PM = B("PM")
        self.ring_ptr = 0
        self.rot = {}
        self.stat_ptr = 0
        self.scr = {}
        self.final_ops = []

    def mark(self, name):
        if not hasattr(self, "marks"):
            self.marks = []
        self.marks.append((name, len(self.S.ops[PE])))

    def rotate(self, key, n):
        v = self.rot.get(key, 0)
        self.rot[key] = (v + 1) % n
        return v

    def stat_cols(self, n):
        if self.stat_ptr + n > 192:
            self.stat_ptr = 0
        a = self.stat_ptr
        self.stat_ptr += n
        return self.stat[:, a:a + n], Buf("stat%d" % a)

    def bank(self, i):
        return self.ps[:, i, :]

    def bank2(self, i):
        return self.ps[:, i:i + 2, :].rearrange("p a b -> p (a b)")

    def ms_begin(self):
        prev = []
        for b in self.b_ms.values():
            if b.w is not None:
                prev.append(b.w)
            prev.extend(b.r)
        if self.b_ms:
            self.ms_prev_ops = prev
        self.b_ms = {}

    def msb(self, name):
        if name not in self.b_ms:
            b = Buf("ms_" + name)
            b.r = list(getattr(self, "ms_prev_ops", []))
            self.b_ms[name] = b
        return self.b_ms[name]

    def scratch_up(self, name, li, Din, Fout):
        key = (name, li)
        if key in self.scr:
            return self.scr[key]
        K = Din // 128
        sc = self.nc.dram_tensor("sc_%s_%d" % (name, li), [128, K, Fout], BF16, kind="Internal").ap()
        f32 = self.w[name][li].rearrange("(k p) f -> p k f", p=128)
        self.scr[key] = (WSrc(f32, sc, key), None)
        return self.scr[key]

    def cast_phase(self, l, ph):
        c = self.c
        D, F = c.D, c.F
        j = l // 4
        if ph == 0:
            self.scratch_up("ff1_w_gate", l, D, F); self.scratch_up("ff1_w_up", l, D, F); self.scratch_up("ff1_w_down", l, F, D)
        elif ph == 1:
            m = l % 4
            if m == 0:
                self.scratch_up("a_w_in", j, D, 2 * c.AH); self.scratch_up("a_w_out", j, c.AH, D)
            elif m == 1:
                self.scratch_grp(j)
            elif m == 2:
                self.scratch_up("c_w_pw1", j, D, 2 * D); self.scratch_up("c_w_pw2", j, D, D)
            else:
                self.scratch_up("d_w_in", j, D, 3 * D); self.scratch_up("d_w_out", j, D, D)
        elif ph == 2:
            self.scratch_up("ff2_w_gate", l, D, F); self.scratch_up("ff2_w_up", l, D, F); self.scratch_up("ff2_w_down", l, F, D)
        else:
            self.scratch_up("ple_w_gate", l, D, D); self.scratch_up("ple_w_proj", l, c.PLE, D)
            self.scratch_p(l)

    def cast_ahead(self, q, look=2):
        L = self.layers
        t = q + look
        if t < 4 * len(L):
            self.cast_phase(L[t // 4], t % 4)

    def scratch_grp(self, j):
        key = ("b_w_grp", j)
        if key in self.scr:
            return self.scr[key]
        c = self.c
        kc = c.PG // 128
        sc = self.nc.dram_tensor("sc_grp_%d" % j, [128, 4, kc, c.PG], BF16, kind="Internal").ap()
        f32 = self.w["b_w_grp"][j].rearrange("g (k p) d -> p g k d", p=128)
        self.scr[key] = (WSrc(f32, sc, key), None)
        return self.scr[key]

    NWB = 16

    def wload(self, src, shape, src_buf):
        c = self.c
        s = self.ring_ptr
        self.ring_ptr = (s + 1) % c.NSLOT
        n = shape[0] * shape[1]
        assert n <= 2048
        dst = self.ring[:, s, 0:n].rearrange("p (a b) -> p a b", a=shape[0])
        if not isinstance(src, WSrc):
            self.S.op(SP, I("dma_start", out=dst, in_=src), reads=[src_buf], writes=[self.b_ring[s]], dma_slot=self.d_ring[s])
            return dst, self.b_ring[s]
        if not hasattr(self, "piece_bufs"):
            self.piece_bufs = {}
            self.piece_owner = {}
            self.wb = [(Buf("wb%d" % i), self.S.new_dma_slot()) for i in range(self.NWB)]
            self.d_ring_sw = [self.S.new_dma_slot() for _ in range(c.NSLOT)]
            self.wb_i = 0
        if src.key not in self.piece_bufs:
            self.piece_bufs[src.key] = Buf("piece")
            self.piece_owner[src.key] = len(self.piece_bufs) % 2 if self.c.NBLK > 1 else 0
        pb = self.piece_bufs[src.key]
        ob = self.piece_owner[src.key]
        if self.blk <= ob:
            self.S.op(POOL, I("dma_start", out=dst, in_=src.f32), writes=[self.b_ring[s]], dma_slot=self.d_ring_sw[s])
            if self.blk == ob:
                wbuf, wslot = self.wb[self.wb_i % self.NWB]
                self.wb_i += 1
                o = self.S.op(SP, I("dma_start", out=src.bf, in_=dst), reads=[self.b_ring[s]], writes=[wbuf], dma_slot=wslot)
                pb.w = o
        else:
            self.S.op(SP, I("dma_start", out=dst, in_=src.bf), reads=[pb], writes=[self.b_ring[s]], dma_slot=self.d_ring[s])
        return dst, self.b_ring[s]

    def setup(self):
        c, S = self.c, self.S
        D, KD = c.D, c.KD
        identf, ident = self.identf, self.ident
        bc = self.b_const
        S.op(POOL, I("memset", identf[:], 0.0), writes=[bc])
        S.op(POOL, I("affine_select", out=identf[:], in_=identf[:], pattern=[[-1, 128]], compare_op=ALU.not_equal,
                                             fill=1.0, base=0, channel_multiplier=1), writes=[bc])
        S.op(POOL, I("tensor_copy", out=ident[:], in_=identf[:]), writes=[bc])
        S.op(POOL, I("memset", self.onesD[:], 1.0 / D), writes=[bc])
        S.op(POOL, I("memset", self.ones_row[:], 1.0), writes=[bc])
        S.op(POOL, I("memset", self.eps_col[:], c.EPS), writes=[bc])
        S.op(POOL, I("memset", self.halo_c[:], 0.0), writes=[self.b_halo_c])
        S.op(POOL, I("memset", self.halo_d[:], 0.0), writes=[self.b_halo_d])
        S.op(POOL, I("memset", self.hs[:, 0, :], 0.0), writes=[self.b_hs[0]])
        vst = self.ms[0:self.NV, 0:D]
        bv = self.msb("setup")
        S.op(POOL, I("memset", vst, 0.0), writes=[bv])
        rows = {}
        r = 0
        slot = S.new_dma_slot()
        ops = []
        for n in ("ff1_pre_g", "mix_pre_g", "ff2_pre_g", "ple_gate_norm_g"):
            rows[n] = r
            ops.append((r, c.DEPTH, self.w[n]))
            r += c.DEPTH
        has = {m: any(l % 4 == m for l in range(c.DEPTH)) for m in range(4)}
        if has[2]:
            for n in ("c_b_dw", "c_norm_g", "c_norm_b"):
                rows[n] = r
                ops.append((r, 1, self.w[n][0:1, :]))
                r += 1
            rows["c_w_dw"] = r
            ops.append((r, c.CW, self.w["c_w_dw"][0]))
            r += c.CW
        if has[3]:
            rows["d_w_conv"] = r
            ops.append((r, c.SW, self.w["d_w_conv"][0]))
            r += c.SW
        assert r <= self.NV
        self.rows = rows
        for i, (r0, n, src) in enumerate(ops):
            S.op(POOL, I("dma_start", out=self.ms[r0:r0 + n, 0:D], in_=src),
                 reads=[bv], writes=[bv] if i == len(ops) - 1 else (), dma_slot=slot)
        NV = self.NV
        bk = self.bank(0)
        for k in range(KD):
            S.op(PE, I("transpose", out=bk[:, k * NV:(k + 1) * NV], in_=self.ms[0:NV, k * 128:(k + 1) * 128],
                                                 identity=identf[0:NV, 0:NV]), reads=[bv, bc], writes=[self.b_ps[0]])
        S.op(DVE, I("tensor_copy", out=self.vecT[:].rearrange("p k v -> p (k v)"), in_=bk[:, 0:KD * NV]),
             writes=[self.b_ps[0], self.b_vecT])
        self.has = has

    def setup_b(self):
        c, S = self.c, self.S
        D, KD = c.D, c.KD
        identf, ident = self.identf, self.ident
        bc = self.b_const
        has = self.has
        if has[0]:
            AG = c.AG
            wsf = self.ms[:, 4096:4096 + AG * 128]
            wsb = self.ms[:, 8192:8192 + AG * 64].bitcast(BF16)
            bw = self.msb("setup_ws")
            S.op(POOL, I("dma_start", out=wsf.rearrange("p (g s) -> p g s", g=AG), in_=self.w["a_w_s"][0].rearrange("g t s -> t g s")),
                 writes=[bw], dma_slot=S.new_dma_slot())
            S.op(POOL, I("affine_select", out=wsf.rearrange("p (g s) -> p g s", g=AG), in_=wsf.rearrange("p (g s) -> p g s", g=AG),
                                                 pattern=[[0, AG], [-1, 128]], compare_op=ALU.is_ge, fill=0.0, base=0, channel_multiplier=1),
                 writes=[bw])
            S.op(POOL, I("tensor_copy", out=wsb, in_=wsf), writes=[bw])
            done = 0
            while done < AG:
                n = min(8, AG - done)
                bi = 1 + (done // 8) % 2
                bkb = self.bank(bi).bitcast(BF16)
                for q in range(n):
                    g = done + q
                    S.op(PE, I("transpose", out=bkb[:, q * 128:(q + 1) * 128], in_=wsb[:, g * 128:(g + 1) * 128], identity=ident[:]),
                         reads=[bw, bc], writes=[self.b_ps[bi]])
                S.op(DVE, I("tensor_copy", out=self.wsT[:, done:done + n, :].rearrange("p g t -> p (g t)"), in_=bkb[:, 0:n * 128]),
                     writes=[self.b_ps[bi], self.b_wsT])
                done += n
            bsf = self.ms[0:1, 10240:10240 + AG * 128]
            bb = self.msb("setup_bs")
            S.op(POOL, I("dma_start", out=bsf, in_=self.w["a_b_s"][0:1].rearrange("o g t -> o (g t)")), writes=[bb], dma_slot=S.new_dma_slot())
            S.op(POOL, I("tensor_copy", out=self.bs_row[:], in_=bsf), reads=[bb], writes=[self.b_wsT])
        if has[1]:
            bp = self.msb("setup_pm")
            t0 = self.ms[:, 2048:2048 + 128]
            t1 = self.ms[:, 2304:2304 + 128]
            for wi, w in enumerate(c.WINS):
                iw = 1.0 / w
                S.op(POOL, I("memset", t0, iw), writes=[bp])
                S.op(POOL, I("affine_select", out=t0, in_=t0, pattern=[[1, 128]], compare_op=ALU.is_ge, fill=0.0, base=0, channel_multiplier=-1), writes=[bp])
                S.op(POOL, I("affine_select", out=t0, in_=t0, pattern=[[-1, 128]], compare_op=ALU.is_ge, fill=0.0, base=w - 1, channel_multiplier=1), writes=[bp])
                S.op(POOL, I("tensor_copy", out=t1, in_=t0), writes=[bp])
                for t in range(w - 1):
                    S.op(POOL, I("memset", t1[0:t + 1, t:t + 1], 1.0 / (t + 1)), writes=[bp])
                S.op(POOL, I("tensor_tensor", out=t1, in0=t1, in1=identf[:], op=ALU.subtract), reads=[bc], writes=[bp])
                S.op(POOL, I("tensor_copy", out=self.PM[:, wi, 2, :], in_=t1), writes=[bp, self.b_PM])
                S.op(POOL, I("tensor_tensor", out=t0, in0=t0, in1=identf[:], op=ALU.subtract), reads=[bc], writes=[bp])
                S.op(POOL, I("tensor_copy", out=self.PM[:, wi, 0, :], in_=t0), writes=[bp, self.b_PM])
                S.op(POOL, I("memset", t0, iw), writes=[bp])
                S.op(POOL, I("affine_select", out=t0, in_=t0, pattern=[[-1, 128]], compare_op=ALU.is_ge, fill=0.0, base=-(129 - w), channel_multiplier=1), writes=[bp])
                S.op(POOL, I("tensor_copy", out=self.PM[:, wi, 1, :], in_=t0), writes=[bp, self.b_PM])

    def setup_fence(self):
        self.ms_begin()

    def vcol(self, name, k, idx=0):
        r = self.rows[name] + idx
        return self.vecT[:, k, r:r + 1]

    def rsqrt(self, x_ap, x_bufs, out_ap, out_buf, n, cscale=1.0, eps=None, scratch=None):
        S = self.S
        eps = self.c.EPS if eps is None else eps
        if scratch is None:
            sc, sb = self.stat_cols(3 * n)
            xx, yy, tt = sc[:, 0:n], sc[:, n:2 * n], sc[:, 2 * n:3 * n]
        else:
            (xx, yy, tt), sb = scratch
        S.op(DVE, I("tensor_single_scalar", out=xx, in_=x_ap, scalar=eps, op=ALU.add), reads=x_bufs, writes=[sb])
        S.op(DVE, I("tensor_scalar", out=yy.bitcast(I32), in0=xx.bitcast(I32), scalar1=1, scalar2=None, op0=ALU.arith_shift_right), writes=[sb])
        S.op(DVE, I("tensor_scalar", out=yy.bitcast(I32), in0=yy.bitcast(I32), scalar1=-1, scalar2=0x5f3759df, op0=ALU.mult, op1=ALU.add), writes=[sb])
        for it in range(2):
            S.op(DVE, I("tensor_tensor", out=tt, in0=yy, in1=yy, op=ALU.mult), writes=[sb])
            S.op(DVE, I("tensor_tensor", out=tt, in0=tt, in1=xx, op=ALU.mult), writes=[sb])
            S.op(DVE, I("tensor_scalar", out=tt, in0=tt, scalar1=-0.5, scalar2=1.5, op0=ALU.mult, op1=ALU.add), writes=[sb])
            if it == 0:
                S.op(DVE, I("tensor_tensor", out=yy, in0=yy, in1=tt, op=ALU.mult), writes=[sb])
            else:
                S.op(DVE, I("scalar_tensor_tensor", out=out_ap, in0=yy, scalar=float(cscale), in1=tt, op0=ALU.mult, op1=ALU.mult),
                     reads=[sb], writes=[out_buf])

    def rsqrt1(self, x_ap, x_bufs, out_ap, out_buf):
        S = self.S
        sc, sb = self.stat_cols(4)
        xx, yy, hx, t = sc[:, 0:1], sc[:, 1:2], sc[:, 2:3], sc[:, 3:4]
        S.op(DVE, I("tensor_single_scalar", out=xx, in_=x_ap, scalar=self.c.EPS, op=ALU.add), reads=x_bufs, writes=[sb])
        S.op(DVE, I("tensor_scalar", out=yy.bitcast(I32), in0=xx.bitcast(I32), scalar1=1, scalar2=None, op0=ALU.arith_shift_right), writes=[sb])
        S.op(DVE, I("tensor_scalar", out=yy.bitcast(I32), in0=yy.bitcast(I32), scalar1=-1, scalar2=0x5f3759df, op0=ALU.mult, op1=ALU.add), writes=[sb])
        S.op(DVE, I("tensor_single_scalar", out=hx, in_=xx, scalar=-0.5, op=ALU.mult), writes=[sb])
        S.op(DVE, I("scalar_tensor_tensor", out=t, in0=yy, scalar=yy, in1=hx, op0=ALU.mult, op1=ALU.mult), writes=[sb])
        S.op(DVE, I("scalar_tensor_tensor", out=yy, in0=t, scalar=1.5, in1=yy, op0=ALU.add, op1=ALU.mult), writes=[sb])
        S.op(DVE, I("scalar_tensor_tensor", out=t, in0=yy, scalar=yy, in1=hx, op0=ALU.mult, op1=ALU.mult), writes=[sb])
        S.op(DVE, I("scalar_tensor_tensor", out=out_ap, in0=t, scalar=1.5, in1=yy, op0=ALU.add, op1=ALU.mult), reads=[sb], writes=[out_buf])

    def rstd(self, x_ap, x_bufs, out_ap, out_buf):
        if getattr(self, "rs_mode", "act") == "dve":
            return self.rsqrt1(x_ap, x_bufs, out_ap, out_buf)
        S = self.S
        sq, b_sq = self.stat_cols(1)
        S.op(ACT, I("activation", out=sq, in_=x_ap, func=AF.Sqrt, bias=self.eps_col[:, 0:1]), reads=list(x_bufs) + [self.b_const], writes=[b_sq])
        S.op(DVE, I("reciprocal", out=out_ap, in_=sq), reads=[b_sq], writes=[out_buf])

    def build_gt(self, gname, l):
        r = self.rows[gname] + l
        for k in range(self.c.KD):
            self.S.op(ACT, I("activation", out=self.gt[:, k, :], in_=self.vecT[:, k, r:r + 1].to_broadcast([128, 128]), func=AF.Copy),
                      reads=[self.b_vecT], writes=[self.b_gt], relax=(k > 0))

    def pre_a(self, tt):
        c, S = self.c, self.S
        msq, b_msq = self.stat_cols(1)
        rs, b_rs = self.stat_cols(1)
        S.op(ACT, I("activation", out=self.hs[:, tt + 1, :], in_=self.h[:, tt, :], func=AF.Square, scale=1.0 / math.sqrt(c.D), accum_out=msq),
             reads=[self.b_h[tt]], writes=[b_msq, self.b_hs[tt + 1]])
        self.rstd(msq, [b_msq], rs, b_rs)
        S.op(ACT, I("activation", out=self.hs[:, tt + 1, :], in_=self.h[:, tt, :], func=AF.Copy, scale=rs),
             reads=[self.b_h[tt], b_rs], writes=[self.b_hs[tt + 1]])

    def pre_b(self, tt):
        c, S = self.c, self.S
        KD = c.KD
        bi = tt % 2
        bkb = self.bank(bi).bitcast(BF16)
        for k in range(KD):
            S.op(PE, I("transpose", out=bkb[:, k * 128:(k + 1) * 128], in_=self.hs[:, tt + 1, k * 128:(k + 1) * 128], identity=self.ident[:]),
                 reads=[self.b_hs[tt + 1], self.b_const], writes=[self.b_ps[bi]])
        S.op(DVE, I("tensor_tensor", out=self.xnT[:, :, tt * 128:(tt + 1) * 128], in0=bkb[:, 0:KD * 128].rearrange("p (k t) -> p k t", k=KD), in1=self.gt[:], op=ALU.mult),
             reads=[self.b_gt], writes=[self.b_ps[bi], self.b_xnT[tt]])

    def prenorm_standalone(self, nxt):
        gname, l, transpose = nxt
        if transpose:
            self.build_gt(gname, l)
        for tt in range(self.c.NT):
            self.pre_a(tt)
            if transpose:
                self.pre_b(tt)

    def phase_begin(self, nxt, rs_mode="act"):
        self.rs_mode = rs_mode
        self.nxt = nxt
        self.pend_b = []
        if nxt is not None and nxt[2]:
            self.build_gt(nxt[0], nxt[1])

    def tile_pe_done(self, tt):
        keep = []
        for t in self.pend_b:
            if t < tt - 1:
                self.pre_b(t)
            else:
                keep.append(t)
        self.pend_b = keep

    def phase_end(self):
        for t in self.pend_b:
            self.pre_b(t)
        self.pend_b = []

    def load_gains(self, l):
        for i, n in enumerate(("ff1_post_g", "mix_post_g", "ff2_post_g", "ple_post_g")):
            src = self.w[n][l:l + 1, :].partition_broadcast(128)
            self.S.op(SP, I("dma_start", out=self.gb[:, i, :], in_=src), writes=[self.b_gb[i]], dma_slot=self.d_gb[i])

    def postnorm(self, tt, src_ap, src_bufs, src_is_psum, gslot, cc, mid=None):
        c, S = self.c, self.S
        D = c.D
        msq, b_msq = self.stat_cols(1)
        rs, b_rs = self.stat_cols(1)
        sr = [] if src_is_psum else list(src_bufs)
        sw = list(src_bufs) if src_is_psum else []
        S.op(ACT, I("activation", out=self.hs[:, tt + 1, :], in_=src_ap, func=AF.Square, scale=1.0 / math.sqrt(D), accum_out=msq),
             reads=sr, writes=sw + [b_msq, self.b_hs[tt + 1]])
        ti = self.rotate("tmp", 2)
        tmp = self.tmp[:, ti, :]
        S.op(DVE, I("scalar_tensor_tensor", out=tmp, in0=src_ap, scalar=float(cc), in1=self.gb[:, gslot, :], op0=ALU.mult, op1=ALU.mult),
             reads=sr + [self.b_gb[gslot]], writes=sw + [self.b_tmp[ti]])
        self.rstd(msq, [b_msq], rs, b_rs)
        S.op(DVE, I("scalar_tensor_tensor", out=self.h[:, tt, :], in0=tmp, scalar=rs, in1=self.h[:, tt, :], op0=ALU.mult, op1=ALU.add),
             reads=[self.b_tmp[ti], b_rs], writes=[self.b_h[tt]])
        if mid is not None:
            mid()
        if self.nxt is not None:
            self.pre_a(tt)
            if self.nxt[2]:
                self.pend_b.append(tt)

    def down_proj(self, nchunks, pieces, gslot, cc):
        c, S = self.c, self.S
        D = c.D
        per = pieces[0][0].shape[1]
        nh = D // 512
        for tt in range(c.NT):
            pair = (6, 4)[self.rotate("dn", 2)]
            for hf in range(nh):
                bi = pair + hf
                for ch in range(nchunks):
                    wap, wbuf = pieces[ch // per]
                    S.op(PE, I("matmul", self.bank(bi), lhsT=self.actT[:, ch, tt * 128:(tt + 1) * 128],
                                                                                    rhs=wap[:, ch % per, hf * 512:(hf + 1) * 512], start=(ch == 0), stop=(ch == nchunks - 1)),
                         reads=[self.b_act[ch], wbuf], writes=[self.b_ps[bi]])
            src = self.bank2(pair) if nh == 2 else self.bank(pair)
            bufs = [self.b_ps[pair + i] for i in range(nh)]
            self.postnorm(tt, src, bufs, True, gslot, cc, mid=lambda tt=tt: self.tile_pe_done(tt))
        self.phase_end()

    def down_pieces(self, name, li, nchunks):
        sc, sbuf = self.scratch_up(name, li, nchunks * 128, self.c.D)
        per = 2048 // self.c.D
        out = []
        for i in range(0, nchunks, per):
            n = min(per, nchunks - i)
            out.append(self.wload(sc[:, i:i + n, :], [n, self.c.D], sbuf))
        return out, per

    def ffn(self, l, which, nxt):
        c, S = self.c, self.S
        D, KD, KF = c.D, c.KD, c.KF
        pre = "ff%d" % which
        self.phase_begin(nxt)
        scg, bg = self.scratch_up(pre + "_w_gate", l, D, c.F)
        scu, bu = self.scratch_up(pre + "_w_up", l, D, c.F)
        xb = self.b_xnT
        KH = KD // 2

        def pair(j, gsel, usel, q, bG, bU):
            for k in range(KD):
                wap, wbuf, kk = gsel(k)
                S.op(PE, I("matmul", self.bank(bG), lhsT=wap[:, kk, q * 128:(q + 1) * 128], rhs=self.xnT[:, k, :], start=(k == 0), stop=(k == KD - 1)),
                     reads=xb + [wbuf], writes=[self.b_ps[bG]])
            for k in range(KD):
                wap, wbuf, kk = usel(k)
                S.op(PE, I("matmul", self.bank(bU), lhsT=wap[:, kk, q * 128:(q + 1) * 128], rhs=self.xnT[:, k, :], start=(k == 0), stop=(k == KD - 1)),
                     reads=xb + [wbuf], writes=[self.b_ps[bU]])
            si = self.rotate("sil", 2)
            S.op(ACT, I("activation", out=self.sil[:, si, :], in_=self.bank(bG), func=AF.Silu), writes=[self.b_ps[bG], self.b_sil[si]])
            S.op(DVE, I("tensor_tensor", out=self.actT[:, j, :], in0=self.sil[:, si, :], in1=self.bank(bU), op=ALU.mult),
                 reads=[self.b_sil[si]], writes=[self.b_ps[bU], self.b_act[j]])

        ngrp = c.F // 512
        for g in range(ngrp):
            c0 = g * 512
            wg = [self.wload(scg[:, kh * KH:(kh + 1) * KH, c0:c0 + 512], [KH, 512], bg) for kh in range(2)]
            wu = [self.wload(scu[:, kh * KH:(kh + 1) * KH, c0:c0 + 512], [KH, 512], bu) for kh in range(2)]
            for q in range(4):
                bG, bU = (2, 3) if q % 2 == 0 else (4, 5)
                pair(g * 4 + q, lambda k: (wg[k // KH][0], wg[k // KH][1], k % KH), lambda k: (wu[k // KH][0], wu[k // KH][1], k % KH), q, bG, bU)
        rem = (c.F - ngrp * 512) // 128
        if rem:
            c0 = ngrp * 512
            wgr = self.wload(scg[:, :, c0:c0 + rem * 128], [KD, rem * 128], bg)
            wur = self.wload(scu[:, :, c0:c0 + rem * 128], [KD, rem * 128], bu)
            for q in range(rem):
                bG, bU = (2, 3) if q % 2 == 0 else (4, 5)
                pair(ngrp * 4 + q, lambda k: (wgr[0], wgr[1], k), lambda k: (wur[0], wur[1], k), q, bG, bU)
        pieces, per = self.down_pieces(pre + "_w_down", l, KF)
        self.down_proj(KF, pieces, 0 if which == 1 else 2, 0.5)

    def up_single(self, sc, sbuf, col0, ncols_chunks, evac):
        c, S = self.c, self.S
        KD = c.KD
        KH = KD // 2
        jj = 0
        while jj < ncols_chunks:
            n = min(4, ncols_chunks - jj)
            if n == 4:
                ws = [self.wload(sc[:, kh * KH:(kh + 1) * KH, col0 + jj * 128:col0 + (jj + 4) * 128], [KH, 512], sbuf) for kh in range(2)]
                sel = lambda k, ws=ws: (ws[k // KH][0], ws[k // KH][1], k % KH)
            else:
                n = min(2, n)
                wr = self.wload(sc[:, :, col0 + jj * 128:col0 + (jj + n) * 128], [KD, n * 128], sbuf)
                sel = lambda k, wr=wr: (wr[0], wr[1], k)
            for q in range(n):
                bi = 2 + self.rotate("up1", 4)
                for k in range(KD):
                    wap, wbuf, kk = sel(k)
                    S.op(PE, I("matmul", self.bank(bi), lhsT=wap[:, kk, q * 128:(q + 1) * 128], rhs=self.xnT[:, k, :], start=(k == 0), stop=(k == KD - 1)),
                         reads=self.b_xnT + [wbuf], writes=[self.b_ps[bi]])
                evac(jj + q, bi)
            jj += n

    def grouped_pieces(self, sc, sb, colA, colB):
        KD = self.c.KD
        KH = KD // 2
        cache = {}

        def sel(j, k):
            g = j // 4
            if g not in cache:
                cache[g] = ([self.wload(sc[:, kh * KH:(kh + 1) * KH, colA + g * 512:colA + (g + 1) * 512], [KH, 512], sb) for kh in range(2)],
                            [self.wload(sc[:, kh * KH:(kh + 1) * KH, colB + g * 512:colB + (g + 1) * 512], [KH, 512], sb) for kh in range(2)])
            pa, pb = cache[g]
            return pa[k // KH], pb[k // KH], k % KH, (j % 4) * 128
        return sel

    def mixer_gmlp(self, l, blk, nxt):
        c, S = self.c, self.S
        self.ms_begin()
        self.phase_begin(nxt)
        D, KD, KA, AH, AG = c.D, c.KD, c.KA, c.AH, c.AG
        j0 = l // 4
        sc, sb = self.scratch_up("a_w_in", j0, D, 2 * AH)
        vg = self.ms[:, 0:AH]
        vb = self.ms[:, AH:2 * AH]
        vt = self.ms[:, 2 * AH:3 * AH]
        vn = [self.ms[:, 3 * AH + i * (AH // 2):3 * AH + (i + 1) * (AH // 2)].bitcast(BF16) for i in range(2)]
        assert (3 * AH + AH) * 4 <= self.MSB
        b_vgb = self.msb("vgb"); b_vt = self.msb("vt"); b_vn = [self.msb("vn0"), self.msb("vn1")]
        vt = [vt, vt]
        b_vt = [b_vt, b_vt]
        if "d_vgb" not in self.__dict__:
            self.d_vgb = self.S.new_dma_slot()
        S.op(SP, I("dma_start", out=vg, in_=self.w["a_v_norm_g"][j0:j0 + 1, :].partition_broadcast(128)), writes=[b_vgb], dma_slot=self.d_vgb)
        S.op(SP, I("dma_start", out=vb, in_=self.w["a_v_norm_b"][j0:j0 + 1, :].partition_broadcast(128)), writes=[b_vgb], dma_slot=self.d_vgb)
        def evac_u(j, bi):
            S.op(ACT, I("activation", out=self.actT[:, j, :], in_=self.bank(bi), func=AF.Gelu_apprx_tanh), writes=[self.b_ps[bi], self.b_act[j]])
        self.up_single(sc, sb, 0, KA, evac_u)
        ncg = AH // 512
        KH = KD // 2
        vp = []
        for cg in range(ncg):
            for kh in range(2):
                vp.append(self.wload(sc[:, kh * KH:(kh + 1) * KH, AH + cg * 512:AH + (cg + 1) * 512], [KH, 512], sb))
        state = {}

        def vpath(tt):
            vi = self.rotate("vn", 2)
            sm, b_sm = self.stat_cols(ncg)
            sq, b_sq = self.stat_cols(ncg)
            for cg in range(ncg):
                bi = 2 + self.rotate("up1", 4)
                for k in range(KD):
                    wap, wbuf = vp[cg * 2 + k // KH]
                    S.op(PE, I("matmul", self.bank(bi), lhsT=self.xnT[:, k, tt * 128:(tt + 1) * 128], rhs=wap[:, k % KH, :], start=(k == 0), stop=(k == KD - 1)),
                         reads=[self.b_xnT[tt], wbuf], writes=[self.b_ps[bi]])
                S.op(ACT, I("activation", out=vt[tt % 2][:, cg * 512:(cg + 1) * 512], in_=self.bank(bi), func=AF.Gelu_apprx_tanh, accum_out=sm[:, cg:cg + 1]),
                     writes=[self.b_ps[bi], b_vt[tt % 2], b_sm])
                S.op(ACT, I("activation", out=vn[vi][:, cg * 512:(cg + 1) * 512], in_=vt[tt % 2][:, cg * 512:(cg + 1) * 512], func=AF.Square, accum_out=sq[:, cg:cg + 1]),
                     reads=[b_vt[tt % 2]], writes=[b_sq, b_vn[vi]])
            st, b_st = self.stat_cols(6)
            mean, ex2, var = st[:, 0:1], st[:, 1:2], st[:, 2:3]
            S.op(DVE, I("tensor_reduce", out=mean, in_=sm, axis=mybir.AxisListType.X, op=ALU.add), reads=[b_sm], writes=[b_st])
            S.op(DVE, I("tensor_reduce", out=ex2, in_=sq, axis=mybir.AxisListType.X, op=ALU.add), reads=[b_sq], writes=[b_st])
            S.op(DVE, I("tensor_scalar", out=st[:, 0:2], in0=st[:, 0:2], scalar1=1.0 / AH, scalar2=None, op0=ALU.mult), writes=[b_st])
            S.op(DVE, I("tensor_tensor", out=var, in0=mean, in1=mean, op=ALU.mult), writes=[b_st])
            S.op(DVE, I("tensor_tensor", out=var, in0=ex2, in1=var, op=ALU.subtract), writes=[b_st])
            S.op(DVE, I("tensor_scalar", out=var, in0=var, scalar1=0.0, scalar2=None, op0=ALU.max), writes=[b_st])
            rs, b_rs = self.stat_cols(1)
            self.rsqrt1(var, [b_st], rs, b_rs)
            v = vt[tt % 2]
            S.op(DVE, I("tensor_scalar", out=v, in0=v, scalar1=mean, scalar2=rs, op0=ALU.subtract, op1=ALU.mult), reads=[b_st, b_rs], writes=[b_vt[tt % 2]])
            S.op(DVE, I("tensor_tensor", out=v, in0=v, in1=vg, op=ALU.mult), reads=[b_vgb], writes=[b_vt[tt % 2]])
            S.op(DVE, I("tensor_tensor", out=vn[vi], in0=v, in1=vb, op=ALU.add), reads=[b_vgb, b_vt[tt % 2]], writes=[b_vn[vi]])
            state[tt] = vi

        def spatial(tt):
            vi = state[tt]
            for c4 in range(KA // 4):
                bi = 2 + self.rotate("up1", 4)
                for q in range(4):
                    ch = c4 * 4 + q
                    g = ch // 2
                    S.op(PE, I("matmul", self.bank(bi)[:, q * 128:(q + 1) * 128], lhsT=self.ones_row[0:1, :], rhs=self.bs_row[0:1, g * 128:(g + 1) * 128], start=True, stop=False),
                         reads=[self.b_const, self.b_wsT], writes=[self.b_ps[bi]])
                    S.op(PE, I("matmul", self.bank(bi)[:, q * 128:(q + 1) * 128], lhsT=vn[vi][:, ch * 128:(ch + 1) * 128], rhs=self.wsT[:, g, :], start=False, stop=True),
                         reads=[b_vn[vi], self.b_wsT], writes=[self.b_ps[bi]])
                ya = self.actT[:, c4 * 4:(c4 + 1) * 4, tt * 128:(tt + 1) * 128]
                S.op(DVE, I("tensor_tensor", out=ya, in0=ya, in1=self.bank(bi).rearrange("p (q t) -> p q t", q=4), op=ALU.mult),
                     writes=[self.b_ps[bi]] + [self.b_act[c4 * 4 + q] for q in range(4)])

        vpath(0)
        for tt in range(1, c.NT):
            vpath(tt)
            spatial(tt - 1)
        spatial(c.NT - 1)
        pieces, per = self.down_pieces("a_w_out", j0, KA)
        self.down_proj(KA, pieces, 1, 1.0)

    def mixer_pool(self, l, blk, nxt):
        c, S = self.c, self.S
        self.ms_begin()
        self.phase_begin(nxt)
        D, KD = c.D, c.KD
        j0 = l // 4
        kc = c.PG // 128
        scg, bg = self.scratch_grp(j0)
        wg, wgb = self.wload(scg.rearrange("p g k d -> p (g k) d"), [4 * kc, c.PG], bg)
        bsc = self.ms[:, 0:D]
        b_bsc = self.msb("bscale")
        if "d_bsc" not in self.__dict__:
            self.d_bsc = S.new_dma_slot()
        S.op(SP, I("dma_start", out=bsc, in_=self.w["b_scale"][j0:j0 + 1, :].partition_broadcast(128)), writes=[b_bsc], dma_slot=self.d_bsc)
        for ch in range(KD):
            wi = ch // kc
            bi = 2 + self.rotate("up1", 4)
            for tt in range(c.NT):
                first = (blk == 0 and tt == 0)
                o = self.bank(bi)[:, tt * 128:(tt + 1) * 128]
                S.op(PE, I("matmul", o, lhsT=self.hs[:, tt + 1, ch * 128:(ch + 1) * 128], rhs=self.PM[:, wi, 2 if first else 0, :], start=True, stop=first),
                     reads=[self.b_hs[tt + 1], self.b_PM], writes=[self.b_ps[bi]])
                if not first:
                    S.op(PE, I("matmul", o, lhsT=self.hs[:, tt, ch * 128:(ch + 1) * 128], rhs=self.PM[:, wi, 1, :], start=False, stop=True),
                         reads=[self.b_hs[tt], self.b_PM], writes=[self.b_ps[bi]])
            S.op(ACT, I("activation", out=self.actT[:, ch, :], in_=self.bank(bi), func=AF.Copy, scale=self.vecT[:, ch, self.rows["mix_pre_g"] + l:self.rows["mix_pre_g"] + l + 1]),
                 reads=[self.b_vecT], writes=[self.b_ps[bi], self.b_act[ch]])
        S.op(ACT, I("activation", out=self.hs[:, 0, :], in_=self.hs[:, c.NT, :], func=AF.Copy), reads=[self.b_hs[c.NT]], writes=[self.b_hs[0]])
        nh = D // 512
        for tt in range(c.NT):
            pair = (6, 4)[self.rotate("dn", 2)]
            for g in range(4):
                col = g * c.PG
                bi = pair + col // 512
                for k in range(kc):
                    S.op(PE, I("matmul", self.bank(bi)[:, col % 512:col % 512 + c.PG], lhsT=self.actT[:, g * kc + k, tt * 128:(tt + 1) * 128],
                                                                                rhs=wg[:, g * kc + k, :], start=(k == 0), stop=(k == kc - 1)),
                         reads=[self.b_act[g * kc + k], wgb], writes=[self.b_ps[bi]])
            self.tile_pe_done(tt)
            src = self.bank2(pair) if nh == 2 else self.bank(pair)
            bufs = [self.b_ps[pair + i] for i in range(nh)]
            yi = self.rotate("ysc", 2)
            ysc = self.ms[:, D * (1 + yi):D * (2 + yi)]
            b_y = self.msb("ysc%d" % yi)
            S.op(DVE, I("tensor_tensor", out=ysc, in0=src, in1=bsc, op=ALU.mult), reads=[b_bsc], writes=bufs + [b_y])
            self.postnorm(tt, ysc, [b_y], False, 1, 1.0)
        self.phase_end()

    def conv_taps(self, eng, zbuf_ap, z_bufs, acc_ap, acc_buf, ch, ntap, wname, bias_name=None):
        S = self.S
        TB = self.c.TB
        for k in range(ntap):
            wk = self.vcol(wname, ch, k)
            if k == 0:
                if bias_name is not None:
                    S.op(eng, I("tensor_scalar", out=acc_ap, in0=zbuf_ap[:, 0:TB], scalar1=wk, scalar2=self.vcol(bias_name, ch), op0=ALU.mult, op1=ALU.add),
                         reads=z_bufs + [self.b_vecT], writes=[acc_buf])
                else:
                    S.op(eng, I("tensor_scalar", out=acc_ap, in0=zbuf_ap[:, 0:TB], scalar1=wk, scalar2=None, op0=ALU.mult),
                         reads=z_bufs + [self.b_vecT], writes=[acc_buf])
            else:
                S.op(eng, I("scalar_tensor_tensor", out=acc_ap, in0=zbuf_ap[:, k:k + TB], scalar=wk, in1=acc_ap, op0=ALU.mult, op1=ALU.add),
                     reads=z_bufs + [self.b_vecT], writes=[acc_buf])

    def scratch_diag(self):
        if "diag" in self.scr:
            return self.scr["diag"]
        c, S = self.c, self.S
        KD, CW = c.KD, c.CW
        sc = self.nc.dram_tensor("sc_diag", [128, KD, CW, 128], BF16, kind="Internal").ap()
        buf = Buf("sc_diag")
        slot = S.new_dma_slot()
        r0 = self.rows["c_w_dw"]
        half = max(1, KD // 2)
        nw = half * CW * 64
        assert nw <= self.MSB // 4
        dg = self.ms[:, 0:nw].bitcast(BF16).rearrange("p (j k m) -> p j k m", j=half, k=CW)
        b_dg = self.msb("dg")
        for j0 in range(0, KD, half):
            for jj in range(half):
                j = j0 + jj
                for k in range(CW):
                    S.op(DVE, I("tensor_scalar", out=dg[:, jj, k, :], in0=self.ident[:], scalar1=self.vecT[:, j, r0 + k:r0 + k + 1], scalar2=None, op0=ALU.mult),
                         reads=[self.b_vecT, self.b_const], writes=[b_dg], relax=(jj + k > 0))
            last = (j0 + half >= KD)
            S.op(POOL, I("dma_start", out=sc[:, j0:j0 + half, :, :], in_=dg), reads=[b_dg], writes=[buf] if last else (), dma_slot=slot)
        self.scr["diag"] = (sc, buf)
        return self.scr["diag"]

    def mixer_conformer(self, l, blk, nxt):
        c, S = self.c, self.S
        self.ms_begin()
        self.phase_begin(nxt)
        D, KD, TB, CW = c.D, c.KD, c.TB, c.CW
        H = CW - 1
        j0 = l // 4
        sc, sb = self.scratch_up("c_w_pw1", j0, D, 2 * D)
        scd, sbd = self.scratch_diag()
        ZW = H + TB
        nz = (KD * ZW + 1) // 2
        z = self.ms[:, 0:nz].bitcast(BF16)[:, 0:KD * ZW].rearrange("p (k w) -> p k w", k=KD)
        o1 = ((nz + 7) // 8) * 8
        acc = self.ms[:, o1:o1 + KD * TB].rearrange("p (k w) -> p k w", k=KD)
        o2 = o1 + KD * TB
        sqt = [self.ms[:, o2 + i * TB:o2 + (i + 1) * TB] for i in range(2)]
        o3 = o2 + 2 * TB
        mean_sb = self.ms[:, o3:o3 + TB]
        rstd_sb = self.ms[:, o3 + TB:o3 + 2 * TB]
        nsc = [self.ms[:, o3 + (2 + i) * TB:o3 + (3 + i) * TB] for i in range(3)]
        assert (o3 + 5 * TB) * 4 <= self.MSB
        b_z = [self.msb("z%d" % k) for k in range(KD)]
        b_acc = [self.msb("acc%d" % k) for k in range(KD)]
        b_sq = [self.msb("sq0"), self.msb("sq1")]
        b_mr = self.msb("meanrstd")
        b_nsc = self.msb("nsc")
        for k in range(KD):
            S.op(ACT, I("activation", out=z[:, k, 0:H], in_=self.halo_c[:, k, :], func=AF.Copy), reads=[self.b_halo_c], writes=[b_z[k]])

        sel_pw1 = self.grouped_pieces(sc, sb, 0, D)

        def pw1(j):
            bA, bG = (2, 3) if j % 2 == 0 else (4, 5)
            for k in range(KD):
                (wp, wb), _, kk, co = sel_pw1(j, k)
                S.op(PE, I("matmul", self.bank(bA), lhsT=wp[:, kk, co:co + 128], rhs=self.xnT[:, k, :], start=(k == 0), stop=(k == KD - 1)),
                     reads=self.b_xnT + [wb], writes=[self.b_ps[bA]])
            for k in range(KD):
                _, (wq, wqb), kk, co = sel_pw1(j, k)
                S.op(PE, I("matmul", self.bank(bG), lhsT=wq[:, kk, co:co + 128], rhs=self.xnT[:, k, :], start=(k == 0), stop=(k == KD - 1)),
                     reads=self.b_xnT + [wqb], writes=[self.b_ps[bG]])
            si = self.rotate("sil", 2)
            S.op(ACT, I("activation", out=self.sil[:, si, :], in_=self.bank(bG), func=AF.Sigmoid), writes=[self.b_ps[bG], self.b_sil[si]])
            S.op(DVE, I("tensor_tensor", out=z[:, j, H:H + TB], in0=self.sil[:, si, :], in1=self.bank(bA), op=ALU.mult),
                 reads=[self.b_sil[si]], writes=[self.b_ps[bA], b_z[j]])
            S.op(ACT, I("activation", out=self.halo_c[:, j, :], in_=z[:, j, TB:TB + H], func=AF.Copy), reads=[b_z[j]], writes=[self.b_halo_c])

        def conv(j):
            n0 = min(16, CW)
            p0 = self.wload(scd[:, j, 0:n0, :], [n0, 128], sbd)
            p1 = self.wload(scd[:, j, n0:CW, :], [CW - n0, 128], sbd) if CW > n0 else None
            bi = 6 + (j % 2)
            for k in range(CW):
                wap, wbuf = p0 if k < n0 else p1
                S.op(PE, I("matmul", self.bank(bi), lhsT=wap[:, k % n0 if k < n0 else k - n0, :], rhs=z[:, j, k:k + TB], start=(k == 0), stop=(k == CW - 1)),
                     reads=[b_z[j], wbuf], writes=[self.b_ps[bi]])
            S.op(ACT, I("activation", out=acc[:, j, :], in_=self.bank(bi), func=AF.Identity, bias=self.vcol("c_b_dw", j)),
                 reads=[self.b_vecT], writes=[self.b_ps[bi], b_acc[j]])
            qi = j % 2
            S.op(ACT, I("activation", out=sqt[qi], in_=acc[:, j, :], func=AF.Square), reads=[b_acc[j]], writes=[b_sq[qi]])

        def stats(j):
            qi = j % 2
            S.op(PE, I("matmul", self.bank(0), lhsT=self.onesD[:], rhs=acc[:, j, :], start=(j == 0), stop=(j == KD - 1)),
                 reads=[b_acc[j], self.b_const], writes=[self.b_ps[0]])
            S.op(PE, I("matmul", self.bank(1), lhsT=self.onesD[:], rhs=sqt[qi], start=(j == 0), stop=(j == KD - 1)),
                 reads=[b_sq[qi], self.b_const], writes=[self.b_ps[1]])

        for j in range(KD):
            pw1(j)
            if j >= 1:
                conv(j - 1)
            if j >= 2:
                stats(j - 2)
        conv(KD - 1)
        if KD >= 2:
            stats(KD - 2)
        stats(KD - 1)
        S.op(DVE, I("tensor_copy", out=mean_sb, in_=self.bank(0)), writes=[self.b_ps[0], b_mr])
        S.op(DVE, I("tensor_tensor", out=nsc[0], in0=mean_sb, in1=mean_sb, op=ALU.mult), reads=[b_mr], writes=[b_nsc])
        S.op(DVE, I("tensor_tensor", out=nsc[0], in0=self.bank(1), in1=nsc[0], op=ALU.subtract), writes=[self.b_ps[1], b_nsc])
        S.op(DVE, I("tensor_scalar", out=nsc[0], in0=nsc[0], scalar1=0.0, scalar2=None, op0=ALU.max), writes=[b_nsc])
        S.op(ACT, I("activation", out=nsc[1], in_=nsc[0], func=AF.Sqrt, bias=self.eps_col[:, 0:1]), reads=[self.b_const], writes=[b_nsc])
        S.op(DVE, I("reciprocal", out=rstd_sb, in_=nsc[1]), reads=[b_nsc], writes=[b_mr])
        for j in range(KD):
            eng = POOL if j % 3 == 2 else DVE
            S.op(eng, I("tensor_tensor", out=acc[:, j, :], in0=acc[:, j, :], in1=mean_sb, op=ALU.subtract), reads=[b_mr], writes=[b_acc[j]])
            S.op(eng, I("tensor_tensor", out=acc[:, j, :], in0=acc[:, j, :], in1=rstd_sb, op=ALU.mult), reads=[b_mr], writes=[b_acc[j]])
            S.op(ACT, I("activation", out=self.actT[:, j, :], in_=acc[:, j, :], func=AF.Silu, scale=self.vcol("c_norm_g", j), bias=self.vcol("c_norm_b", j)),
                 reads=[b_acc[j], self.b_vecT], writes=[self.b_act[j]])
        pieces, per = self.down_pieces("c_w_pw2", j0, KD)
        self.down_proj(KD, pieces, 1, 1.0)

    def mixer_shortconv(self, l, blk, nxt):
        c, S = self.c, self.S
        self.ms_begin()
        self.phase_begin(nxt)
        D, KD, TB, SW = c.D, c.KD, c.TB, c.SW
        H = SW - 1
        j0 = l // 4
        sc, sb = self.scratch_up("d_w_in", j0, D, 3 * D)
        ZW = H + TB
        m = self.ms[:, 0:KD * ZW].rearrange("p (k w) -> p k w", k=KD)
        acc = self.ms[:, KD * ZW:KD * ZW + KD * TB].rearrange("p (k w) -> p k w", k=KD)
        assert (KD * ZW + KD * TB) * 4 <= self.MSB
        b_m = [self.msb("m%d" % k) for k in range(KD)]
        b_acc = [self.msb("sacc%d" % k) for k in range(KD)]
        for k in range(KD):
            S.op(ACT, I("activation", out=m[:, k, 0:H], in_=self.halo_d[:, k, :], func=AF.Copy), reads=[self.b_halo_d], writes=[b_m[k]])
        sel_in = self.grouped_pieces(sc, sb, D, 2 * D)
        for j in range(KD):
            bC, bX = (2, 3) if j % 2 == 0 else (4, 5)
            for k in range(KD):
                (wp, wb), _, kk, co = sel_in(j, k)
                S.op(PE, I("matmul", self.bank(bC), lhsT=wp[:, kk, co:co + 128], rhs=self.xnT[:, k, :], start=(k == 0), stop=(k == KD - 1)),
                     reads=self.b_xnT + [wb], writes=[self.b_ps[bC]])
            for k in range(KD):
                _, (wq, wqb), kk, co = sel_in(j, k)
                S.op(PE, I("matmul", self.bank(bX), lhsT=wq[:, kk, co:co + 128], rhs=self.xnT[:, k, :], start=(k == 0), stop=(k == KD - 1)),
                     reads=self.b_xnT + [wqb], writes=[self.b_ps[bX]])
            si = self.rotate("sil", 2)
            S.op(ACT, I("activation", out=self.sil[:, si, :], in_=self.bank(bC), func=AF.Copy), writes=[self.b_ps[bC], self.b_sil[si]])
            S.op(DVE, I("tensor_tensor", out=m[:, j, H:H + TB], in0=self.sil[:, si, :], in1=self.bank(bX), op=ALU.mult),
                 reads=[self.b_sil[si]], writes=[self.b_ps[bX], b_m[j]])
            eng = DVE
            self.conv_taps(eng, m[:, j, :], [b_m[j]], acc[:, j, :], b_acc[j], j, SW, "d_w_conv")
            S.op(ACT, I("activation", out=self.halo_d[:, j, :], in_=m[:, j, TB:TB + H], func=AF.Copy), reads=[b_m[j]], writes=[self.b_halo_d])

        def evac_bg(j, bi):
            S.op(DVE, I("tensor_tensor", out=self.actT[:, j, :], in0=acc[:, j, :], in1=self.bank(bi), op=ALU.mult),
                 reads=[b_acc[j]], writes=[self.b_ps[bi], self.b_act[j]])
        self.up_single(sc, sb, 0, KD, evac_bg)
        pieces, per = self.down_pieces("d_w_out", j0, KD)
        self.down_proj(KD, pieces, 1, 1.0)

    def ple(self, l, blk, nxt):
        c, S = self.c, self.S
        self.ms_begin()
        self.phase_begin(nxt, rs_mode="act")
        D, KD, KP, TB = c.D, c.KD, c.KP, c.TB
        o0 = 2 * D
        n1 = c.NT * c.PLE // 2
        n2 = KP * TB // 2
        p_sb = self.ms[:, o0:o0 + n1].bitcast(BF16).rearrange("p (t c) -> p t c", t=c.NT)
        pT = self.ms[:, o0 + n1:o0 + n1 + n2].bitcast(BF16).rearrange("p (k t) -> p k t", k=KP)
        b_p_sb = self.msb("p_sb")
        b_pT = self.msb("pT")
        S.op(POOL, I("dma_start", out=p_sb, in_=self.p[l, blk * TB:(blk + 1) * TB, :].rearrange("(t q) c -> q t c", q=128)),
             writes=[b_p_sb], dma_slot=self.d_p)
        bi = 1
        bkb = self.bank(bi).bitcast(BF16)
        for tt in range(c.NT):
            for k in range(KP):
                S.op(PE, I("transpose", out=bkb[:, k * TB + tt * 128:k * TB + (tt + 1) * 128], in_=p_sb[:, tt, k * 128:(k + 1) * 128], identity=self.ident[:]),
                     reads=[b_p_sb, self.b_const], writes=[self.b_ps[bi]])
        S.op(DVE, I("tensor_copy", out=pT.rearrange("p k t -> p (k t)"), in_=bkb[:, 0:KP * TB]), writes=[self.b_ps[bi], b_pT])
        scg, bg = self.scratch_up("ple_w_gate", l, D, D)
        scj, bj = self.scratch_up("ple_w_proj", l, c.PLE, D)
        nh = D // 512
        gp = {}
        KH = KD // 2
        for hf in range(nh):
            for kh in range(2):
                gp[(hf, kh)] = self.wload(scg[:, kh * KH:(kh + 1) * KH, hf * 512:(hf + 1) * 512], [KH, 512], bg)
        pj = [self.wload(scj[:, :, hf * 512:(hf + 1) * 512], [KP, 512], bj) for hf in range(nh)]
        ets = []
        for tt in range(c.NT):
            oe = 2 * D + 1024 + tt * D
            et = self.ms[:, oe:oe + D]
            b_e = self.msb("etile%d" % tt)
            ets.append((et, b_e))
            for hf in range(nh):
                bG, bP = (2, 3) if (tt * nh + hf) % 2 == 0 else (4, 5)
                for k in range(KD):
                    wap, wbuf = gp[(hf, k // KH)]
                    S.op(PE, I("matmul", self.bank(bG), lhsT=self.xnT[:, k, tt * 128:(tt + 1) * 128], rhs=wap[:, k % KH, :], start=(k == 0), stop=(k == KD - 1)),
                         reads=[self.b_xnT[tt], wbuf], writes=[self.b_ps[bG]])
                wap, wbuf = pj[hf]
                for k in range(KP):
                    S.op(PE, I("matmul", self.bank(bP), lhsT=pT[:, k, tt * 128:(tt + 1) * 128], rhs=wap[:, k, :], start=(k == 0), stop=(k == KP - 1)),
                         reads=[b_pT, wbuf], writes=[self.b_ps[bP]])
                si = self.rotate("sil", 2)
                S.op(ACT, I("activation", out=self.sil[:, si, :], in_=self.bank(bG), func=AF.Sigmoid), writes=[self.b_ps[bG], self.b_sil[si]])
                S.op(DVE, I("tensor_tensor", out=et[:, hf * 512:(hf + 1) * 512], in0=self.sil[:, si, :], in1=self.bank(bP), op=ALU.mult),
                     reads=[self.b_sil[si]], writes=[self.b_ps[bP], b_e])
        for tt in range(c.NT):
            et, b_e = ets[tt]
            self.postnorm(tt, et, [b_e], False, 3, 1.0)
        self.phase_end()

    def build(self):
        c, S = self.c, self.S
        L = self.layers
        self.setup()
        for blk in range(c.NBLK):
            self.blk = blk
            for tt in range(c.NT):
                r0 = (blk * c.NT + tt) * 128
                S.op(SP, I("dma_start", out=self.h[:, tt, :], in_=self.x[r0:r0 + 128, :]), writes=[self.b_h[tt]], dma_slot=self.d_x[tt])
            if blk == 0:
                self.load_gains(L[0])
                if 2 in [l % 4 for l in L]:
                    self.scratch_diag()
                    self.setup_fence()
                self.setup_b()
                self.setup_fence()
            self.prenorm_standalone(("ff1_pre_g", L[0], True))
            for i, l in enumerate(L):
                if not (blk == 0 and i == 0):
                    self.load_gains(l)
                m = l % 4
                self.mark("b%d_l%d_ffn1" % (blk, l))
                self.ffn(l, 1, ("mix_pre_g", l, m != 1))
                self.mark("b%d_l%d_mix" % (blk, l))
                nx = ("ff2_pre_g", l, True)
                if m == 0:
                    self.mixer_gmlp(l, blk, nx)
                elif m == 1:
                    self.mixer_pool(l, blk, nx)
                elif m == 2:
                    self.mixer_conformer(l, blk, nx)
                else:
                    self.mixer_shortconv(l, blk, nx)
                self.mark("b%d_l%d_ffn2" % (blk, l))
                self.ffn(l, 2, ("ple_gate_norm_g", l, True))
                self.mark("b%d_l%d_ple" % (blk, l))
                self.ple(l, blk, ("ff1_pre_g", L[i + 1], True) if i + 1 < len(L) else None)
            for tt in range(c.NT):
                r0 = (blk * c.NT + tt) * 128
                o = S.op(SP, I("dma_start", out=self.y[r0:r0 + 128, :], in_=self.h[:, tt, :]), reads=[self.b_h[tt]], dma_slot=self.d_y[tt])
                self.final_ops.append(o)
        S.finalize_and_emit(final_waits=self.final_ops[-c.NT:])
        return self.nc


LAUNCH_GROUPS = [[0, 1, 2, 3]]
_NC_CACHE = {}


def _get_nc(cfg, layers):
    key = (id(cfg), tuple(layers))
    if key not in _NC_CACHE:
        _NC_CACHE[key] = Builder(cfg, layers).build()
    return _NC_CACHE[key]


def kernel(**inputs):
    c = FULL
    n_cores = 8
    x = np.ascontiguousarray(inputs["x"], dtype=np.float32)
    p = inputs["p"]
    wts = {n: np.ascontiguousarray(inputs[n], dtype=np.float32) for n in WEIGHT_NAMES}
    p_sh = [np.ascontiguousarray(p[:, b]) for b in range(n_cores)]
    cur = [x[b] for b in range(n_cores)]
    for layers in LAUNCH_GROUPS:
        nc = _get_nc(c, layers)
        in_maps = []
        for b in range(n_cores):
            m = {"x": cur[b], "p": p_sh[b]}
            m.update(wts)
            in_maps.append(m)
        res = run_bass_kernel_spmd(nc, in_maps, core_ids=list(range(n_cores)))
        cur = [np.asarray(r["y"]) for r in res.results]
    return np.stack(cur, axis=0).astype(np.float32)
```

```python
import contextlib
import math
import numpy as np
import concourse.bass as bass
import concourse.mybir as mybir
from concourse.bass_utils import run_bass_kernel_spmd

F32 = mybir.dt.float32
BF16 = mybir.dt.bfloat16
I32 = mybir.dt.int32
AF = mybir.ActivationFunctionType
ALU = mybir.AluOpType

PE, ACT, DVE, POOL, SP = "tensor", "scalar", "vector", "gpsimd", "sync"
ENGINES = (PE, ACT, DVE, POOL, SP)


def I(name, *a, **kw):
    return lambda e: getattr(e, name)(*a, **kw)


class WSrc:
    def __init__(self, f32, bf, key):
        self.f32, self.bf, self.key = f32, bf, key

    def __getitem__(self, idx):
        return WSrc(self.f32[idx], self.bf[idx], self.key + (repr(idx),))

    def rearrange(self, pat, **kw):
        return WSrc(self.f32.rearrange(pat, **kw), self.bf.rearrange(pat, **kw), self.key + (pat,))


class Buf:
    __slots__ = ("name", "w", "r")

    def __init__(self, name):
        self.name = name
        self.w = None
        self.r = []


class Op:
    __slots__ = ("eng", "fn", "deps", "idx", "is_dma", "sem", "val", "needs_inc")

    def __init__(self, eng, fn, is_dma):
        self.eng = eng
        self.fn = fn
        self.deps = []
        self.is_dma = is_dma
        self.sem = None
        self.val = None
        self.needs_inc = is_dma


class Sched:
    def __init__(self, nc):
        self.nc = nc
        self.ops = {e: [] for e in ENGINES}
        self.dma_sems = []
        self.eng_sem = {}

    def _add_dep(self, o, d):
        if d is None or d is o:
            return
        if (not d.is_dma) and d.eng == PE and o.eng == PE and not o.is_dma:
            return
        o.deps.append(d)

    def op(self, eng, fn, reads=(), writes=(), dma_slot=None, relax=False):
        is_dma = dma_slot is not None
        o = Op(eng, fn, is_dma)
        o.idx = len(self.ops[eng])
        for b in reads:
            self._add_dep(o, b.w)
        for b in writes:
            self._add_dep(o, b.w)
            for r in b.r:
                self._add_dep(o, r)
        if relax:
            o.deps = [d for d in o.deps if d.is_dma or d.eng != eng]
        for b in reads:
            b.r.append(o)
        for b in writes:
            b.w = o
            b.r = []
        if is_dma:
            dma_slot[1] += 16
            o.sem = dma_slot
            o.val = dma_slot[1]
        self.ops[eng].append(o)
        return o

    def new_dma_slot(self):
        s = [None, 0]
        self.dma_sems.append(s)
        return s

    def finalize_and_emit(self, final_waits=()):
        nc = self.nc
        for e in ENGINES:
            for o in self.ops[e]:
                best = {}
                dmas = {}
                for d in o.deps:
                    if d.is_dma:
                        k = id(d.sem)
                        if k not in dmas or dmas[k].val < d.val:
                            dmas[k] = d
                    else:
                        k = d.eng
                        if k not in best or best[k].idx < d.idx:
                            best[k] = d
                o.deps = list(best.values()) + list(dmas.values())
                for d in best.values():
                    d.needs_inc = True
        for d in final_waits:
            d.needs_inc = True
        for e in ENGINES:
            c = 0
            for o in self.ops[e]:
                if not o.is_dma and o.needs_inc:
                    c += 1
                    o.val = c
        with contextlib.ExitStack() as st:
            for e in (PE, ACT, DVE, POOL):
                self.eng_sem[e] = st.enter_context(nc.semaphore("s_" + e))
            for i, s in enumerate(self.dma_sems):
                s[0] = st.enter_context(nc.semaphore("d%d" % i))
            block = st.enter_context(nc.Block())
            sched = self

            def emit(e, eng):
                waited = {}
                for o in sched.ops[e]:
                    for d in o.deps:
                        if d.is_dma:
                            sem, val, key = d.sem[0], d.val, id(d.sem)
                        else:
                            sem, val, key = sched.eng_sem[d.eng], d.val, d.eng
                        if waited.get(key, 0) < val:
                            eng.wait_ge(sem, val)
                            waited[key] = val
                    ins = o.fn(eng)
                    if o.is_dma:
                        ins.then_inc(o.sem[0], 16)
                    elif o.needs_inc:
                        ins.then_inc(sched.eng_sem[e], 1)
                if e == SP:
                    for d in final_waits:
                        if d.is_dma:
                            eng.wait_ge(d.sem[0], d.val)
                        else:
                            eng.wait_ge(sched.eng_sem[d.eng], d.val)

            @block.tensor
            def _(eng):
                emit(PE, eng)

            @block.scalar
            def _(eng):
                emit(ACT, eng)

            @block.vector
            def _(eng):
                emit(DVE, eng)

            @block.gpsimd
            def _(eng):
                emit(POOL, eng)

            @block.sync
            def _(eng):
                emit(SP, eng)


class Cfg:
    def __init__(self, D=1024, F=2816, S=4096, PLE=256, DEPTH=4, CW=31, SW=3, NSLOT=13):
        self.D, self.F, self.S, self.PLE, self.DEPTH, self.CW, self.SW = D, F, S, PLE, DEPTH, CW, SW
        self.AH = 3 * D
        self.AG = self.AH // 256
        self.PG = D // 4
        self.KD = D // 128
        self.KF = F // 128
        self.KA = self.AH // 128
        self.KP = PLE // 128
        self.TB = 512
        self.NT = 4
        self.NBLK = S // self.TB
        self.NSLOT = NSLOT
        self.WINS = (2, 4, 8, 16)
        self.EPS = 1e-6
        assert self.PG % 128 == 0 and F % 256 == 0 and self.KA % 4 == 0


FULL = Cfg()

WEIGHT_NAMES = [
    "ff1_pre_g", "ff1_w_gate", "ff1_w_up", "ff1_w_down", "ff1_post_g", "mix_pre_g", "mix_post_g",
    "ff2_pre_g", "ff2_w_gate", "ff2_w_up", "ff2_w_down", "ff2_post_g",
    "ple_gate_norm_g", "ple_w_gate", "ple_w_proj", "ple_post_g",
    "a_w_in", "a_v_norm_g", "a_v_norm_b", "a_w_s", "a_b_s", "a_w_out", "b_w_grp", "b_scale",
    "c_w_pw1", "c_w_dw", "c_b_dw", "c_norm_g", "c_norm_b", "c_w_pw2", "d_w_in", "d_w_conv", "d_w_out",
]


def weight_shapes(c):
    L, D, F = c.DEPTH, c.D, c.F
    nA = len(range(0, L, 4)); nB = len(range(1, L, 4)); nC = len(range(2, L, 4)); nD = len(range(3, L, 4))
    return {
        "ff1_pre_g": (L, D), "ff1_w_gate": (L, D, F), "ff1_w_up": (L, D, F), "ff1_w_down": (L, F, D),
        "ff1_post_g": (L, D), "mix_pre_g": (L, D), "mix_post_g": (L, D), "ff2_pre_g": (L, D),
        "ff2_w_gate": (L, D, F), "ff2_w_up": (L, D, F), "ff2_w_down": (L, F, D), "ff2_post_g": (L, D),
        "ple_gate_norm_g": (L, D), "ple_w_gate": (L, D, D), "ple_w_proj": (L, c.PLE, D), "ple_post_g": (L, D),
        "a_w_in": (nA, D, 2 * c.AH), "a_v_norm_g": (nA, c.AH), "a_v_norm_b": (nA, c.AH),
        "a_w_s": (nA, c.AG, 128, 128), "a_b_s": (nA, c.AG, 128), "a_w_out": (nA, c.AH, D),
        "b_w_grp": (nB, 4, c.PG, c.PG), "b_scale": (nB, D),
        "c_w_pw1": (nC, D, 2 * D), "c_w_dw": (nC, c.CW, D), "c_b_dw": (nC, D), "c_norm_g": (nC, D),
        "c_norm_b": (nC, D), "c_w_pw2": (nC, D, D),
        "d_w_in": (nD, D, 3 * D), "d_w_conv": (nD, c.SW, D), "d_w_out": (nD, D, D),
    }


class Builder:
    def __init__(self, c, layers):
        self.c = c
        self.layers = list(layers)
        nc = self.nc = bass.Bass("TRN2", target_bir_lowering=False)
        self.S = Sched(nc)
        D, KD = c.D, c.KD
        self.x = nc.dram_tensor("x", [c.S, D], F32, kind="ExternalInput").ap()
        self.p = nc.dram_tensor("p", [c.DEPTH, c.S, c.PLE], F32, kind="ExternalInput").ap()
        self.y = nc.dram_tensor("y", [c.S, D], F32, kind="ExternalOutput").ap()
        self.w = {}
        for n, shp in weight_shapes(c).items():
            self.w[n] = nc.dram_tensor(n, list(shp), F32, kind="ExternalInput").ap()
        A = nc.alloc_sbuf_tensor
        self.h = A("h", [128, c.NT, D], F32)
        self.hs = A("hs", [128, c.NT + 1, D], BF16)
        self.xnT = A("xnT", [128, KD, c.TB], BF16)
        self.NACT = max(c.KA, c.KF, KD)
        self.actT = A("actT", [128, self.NACT, c.TB], BF16)
        self.ring = A("ring", [128, c.NSLOT, 2048], BF16)
        self.gb = A("gb", [128, 4, D], F32)
        self.NV = 64
        self.vecT = A("vecT", [128, KD, self.NV], F32)
        self.ident = A("ident", [128, 128], BF16)
        self.identf = A("identf", [128, 128], F32)
        self.onesD = A("onesD", [128, 128], F32)
        self.ones_row = A("ones_row", [1, 128], BF16)
        self.sil = A("sil", [128, 2, c.TB], F32)
        self.tmp = A("tmp", [128, 2, D], F32)
        self.stat = A("stat", [128, 192], F32)
        self.MSB = 48 * 1024
        self.ms = A("ms", [128, self.MSB // 4], F32)
        self.halo_c = A("halo_c", [128, KD, c.CW - 1], BF16)
        self.gt = A("gt", [128, KD, 128], F32)
        self.eps_col = A("eps_col", [128, 8], F32)
        self.dg = None
        self.b_dg = Buf("dg")
        self.b_gt = Buf("gt")
        self.halo_d = A("halo_d", [128, KD, c.SW - 1], F32)
        self.wsT = A("wsT", [128, c.AG, 128], BF16)
        self.bs_row = A("bs_row", [1, c.AG * 128], BF16)
        self.PM = A("PM", [128, 4, 3, 128], BF16)
        self.fence_t = A("fence_t", [128, 8], F32)
        self.ps = nc.alloc_psum_tensor("ps", [128, 8, 512], F32)
        B = Buf
        self.b_h = [B("h%d" % i) for i in range(c.NT)]
        self.b_hs = [B("hs%d" % i) for i in range(c.NT + 1)]
        self.b_xnT = [B("xnT%d" % i) for i in range(c.NT)]
        self.b_act = [B("act%d" % i) for i in range(self.NACT)]
        self.b_ring = [B("ring%d" % i) for i in range(c.NSLOT)]
        self.d_ring = [self.S.new_dma_slot() for _ in range(c.NSLOT)]
        self.b_gb = [B("gb%d" % i) for i in range(4)]
        self.d_gb = [self.S.new_dma_slot() for _ in range(4)]
        self.b_ps = [B("ps%d" % i) for i in range(8)]
        self.b_sil = [B("sil0"), B("sil1")]
        self.b_tmp = [B("tmp0"), B("tmp1")]
        self.b_const = B("const")
        self.b_vecT = B("vecT")
        self.d_p = self.S.new_dma_slot()
        self.b_ms = {}
        self.d_x = [self.S.new_dma_slot() for _ in range(c.NT)]
        self.d_y = [self.S.new_dma_slot() for _ in range(c.NT)]
        self.b_halo_c = B("halo_c")
        self.b_halo_d = B("halo_d")
        self.b_wsT = B("wsT")
        self.b_PM = B("PM")
        self.ring_ptr = 0
        self.rot = {}
        self.stat_ptr = 0
        self.scr = {}
        self.final_ops = []

    def mark(self, name):
        if not hasattr(self, "marks"):
            self.marks = []
        self.marks.append((name, len(self.S.ops[PE])))

    def rotate(self, key, n):
        v = self.rot.get(key, 0)
        self.rot[key] = (v + 1) % n
        return v

    def stat_cols(self, n):
        if self.stat_ptr + n > 192:
            self.stat_ptr = 0
        a = self.stat_ptr
        self.stat_ptr += n
        return self.stat[:, a:a + n], Buf("stat%d" % a)

    def bank(self, i):
        return self.ps[:, i, :]

    def bank2(self, i):
        return self.ps[:, i:i + 2, :].rearrange("p a b -> p (a b)")

    def ms_begin(self):
        prev = []
        for b in self.b_ms.values():
            if b.w is not None:
                prev.append(b.w)
            prev.extend(b.r)
        if self.b_ms:
            self.ms_prev_ops = prev
        self.b_ms = {}

    def msb(self, name):
        if name not in self.b_ms:
            b = Buf("ms_" + name)
            b.r = list(getattr(self, "ms_prev_ops", []))
            self.b_ms[name] = b
        return self.b_ms[name]

    def scratch_up(self, name, li, Din, Fout):
        key = (name, li)
        if key in self.scr:
            return self.scr[key]
        K = Din // 128
        sc = self.nc.dram_tensor("sc_%s_%d" % (name, li), [128, K, Fout], BF16, kind="Internal").ap()
        f32 = self.w[name][li].rearrange("(k p) f -> p k f", p=128)
        self.scr[key] = (WSrc(f32, sc, key), None)
        return self.scr[key]

    def cast_phase(self, l, ph):
        c = self.c
        D, F = c.D, c.F
        j = l // 4
        if ph == 0:
            self.scratch_up("ff1_w_gate", l, D, F); self.scratch_up("ff1_w_up", l, D, F); self.scratch_up("ff1_w_down", l, F, D)
        elif ph == 1:
            m = l % 4
            if m == 0:
                self.scratch_up("a_w_in", j, D, 2 * c.AH); self.scratch_up("a_w_out", j, c.AH, D)
            elif m == 1:
                self.scratch_grp(j)
            elif m == 2:
                self.scratch_up("c_w_pw1", j, D, 2 * D); self.scratch_up("c_w_pw2", j, D, D)
            else:
                self.scratch_up("d_w_in", j, D, 3 * D); self.scratch_up("d_w_out", j, D, D)
        elif ph == 2:
            self.scratch_up("ff2_w_gate", l, D, F); self.scratch_up("ff2_w_up", l, D, F); self.scratch_up("ff2_w_down", l, F, D)
        else:
            self.scratch_up("ple_w_gate", l, D, D); self.scratch_up("ple_w_proj", l, c.PLE, D)
            self.scratch_p(l)

    def cast_ahead(self, q, look=2):
        L = self.layers
        t = q + look
        if t < 4 * len(L):
            self.cast_phase(L[t // 4], t % 4)

    def scratch_grp(self, j):
        key = ("b_w_grp", j)
        if key in self.scr:
            return self.scr[key]
        c = self.c
        kc = c.PG // 128
        sc = self.nc.dram_tensor("sc_grp_%d" % j, [128, 4, kc, c.PG], BF16, kind="Internal").ap()
        f32 = self.w["b_w_grp"][j].rearrange("g (k p) d -> p g k d", p=128)
        self.scr[key] = (WSrc(f32, sc, key), None)
        return self.scr[key]

    NWB = 16

    def wload(self, src, shape, src_buf):
        c = self.c
        s = self.ring_ptr
        self.ring_ptr = (s + 1) % c.NSLOT
        n = shape[0] * shape[1]
        assert n <= 2048
        dst = self.ring[:, s, 0:n].rearrange("p (a b) -> p a b", a=shape[0])
        if not isinstance(src, WSrc):
            self.S.op(SP, I("dma_start", out=dst, in_=src), reads=[src_buf], writes=[self.b_ring[s]], dma_slot=self.d_ring[s])
            return dst, self.b_ring[s]
        if not hasattr(self, "piece_bufs"):
            self.piece_bufs = {}
            self.wb = [(Buf("wb%d" % i), self.S.new_dma_slot()) for i in range(self.NWB)]
            self.d_ring_sw = [self.S.new_dma_slot() for _ in range(c.NSLOT)]
            self.wb_i = 0
        pb = self.piece_bufs.setdefault(src.key, Buf("piece"))
        if self.blk == 0:
            self.S.op(POOL, I("dma_start", out=dst, in_=src.f32), writes=[self.b_ring[s]], dma_slot=self.d_ring_sw[s])
            wbuf, wslot = self.wb[self.wb_i % self.NWB]
            self.wb_i += 1
            o = self.S.op(SP, I("dma_start", out=src.bf, in_=dst), reads=[self.b_ring[s]], writes=[wbuf], dma_slot=wslot)
            pb.w = o
        else:
            self.S.op(SP, I("dma_start", out=dst, in_=src.bf), reads=[pb], writes=[self.b_ring[s]], dma_slot=self.d_ring[s])
        return dst, self.b_ring[s]

    def setup(self):
        c, S = self.c, self.S
        D, KD = c.D, c.KD
        identf, ident = self.identf, self.ident
        bc = self.b_const
        S.op(POOL, I("memset", identf[:], 0.0), writes=[bc])
        S.op(POOL, I("affine_select", out=identf[:], in_=identf[:], pattern=[[-1, 128]], compare_op=ALU.not_equal,
                                             fill=1.0, base=0, channel_multiplier=1), writes=[bc])
        S.op(POOL, I("tensor_copy", out=ident[:], in_=identf[:]), writes=[bc])
        S.op(POOL, I("memset", self.onesD[:], 1.0 / D), writes=[bc])
        S.op(POOL, I("memset", self.ones_row[:], 1.0), writes=[bc])
        S.op(POOL, I("memset", self.eps_col[:], c.EPS), writes=[bc])
        S.op(POOL, I("memset", self.halo_c[:], 0.0), writes=[self.b_halo_c])
        S.op(POOL, I("memset", self.halo_d[:], 0.0), writes=[self.b_halo_d])
        S.op(POOL, I("memset", self.hs[:, 0, :], 0.0), writes=[self.b_hs[0]])
        vst = self.ms[0:self.NV, 0:D]
        bv = self.msb("setup")
        S.op(POOL, I("memset", vst, 0.0), writes=[bv])
        rows = {}
        r = 0
        slot = S.new_dma_slot()
        ops = []
        for n in ("ff1_pre_g", "mix_pre_g", "ff2_pre_g", "ple_gate_norm_g"):
            rows[n] = r
            ops.append((r, c.DEPTH, self.w[n]))
            r += c.DEPTH
        has = {m: any(l % 4 == m for l in range(c.DEPTH)) for m in range(4)}
        if has[2]:
            for n in ("c_b_dw", "c_norm_g", "c_norm_b"):
                rows[n] = r
                ops.append((r, 1, self.w[n][0:1, :]))
                r += 1
            rows["c_w_dw"] = r
            ops.append((r, c.CW, self.w["c_w_dw"][0]))
            r += c.CW
        if has[3]:
            rows["d_w_conv"] = r
            ops.append((r, c.SW, self.w["d_w_conv"][0]))
            r += c.SW
        assert r <= self.NV
        self.rows = rows
        for i, (r0, n, src) in enumerate(ops):
            S.op(POOL, I("dma_start", out=self.ms[r0:r0 + n, 0:D], in_=src),
                 reads=[bv], writes=[bv] if i == len(ops) - 1 else (), dma_slot=slot)
        NV = self.NV
        bk = self.bank(0)
        for k in range(KD):
            S.op(PE, I("transpose", out=bk[:, k * NV:(k + 1) * NV], in_=self.ms[0:NV, k * 128:(k + 1) * 128],
                                                 identity=identf[0:NV, 0:NV]), reads=[bv, bc], writes=[self.b_ps[0]])
        S.op(DVE, I("tensor_copy", out=self.vecT[:].rearrange("p k v -> p (k v)"), in_=bk[:, 0:KD * NV]),
             writes=[self.b_ps[0], self.b_vecT])
        self.has = has

    def setup_b(self):
        c, S = self.c, self.S
        D, KD = c.D, c.KD
        identf, ident = self.identf, self.ident
        bc = self.b_const
        has = self.has
        if has[0]:
            AG = c.AG
            wsf = self.ms[:, 4096:4096 + AG * 128]
            wsb = self.ms[:, 8192:8192 + AG * 64].bitcast(BF16)
            bw = self.msb("setup_ws")
            S.op(POOL, I("dma_start", out=wsf.rearrange("p (g s) -> p g s", g=AG), in_=self.w["a_w_s"][0].rearrange("g t s -> t g s")),
                 writes=[bw], dma_slot=S.new_dma_slot())
            S.op(POOL, I("affine_select", out=wsf.rearrange("p (g s) -> p g s", g=AG), in_=wsf.rearrange("p (g s) -> p g s", g=AG),
                                                 pattern=[[0, AG], [-1, 128]], compare_op=ALU.is_ge, fill=0.0, base=0, channel_multiplier=1),
                 writes=[bw])
            S.op(POOL, I("tensor_copy", out=wsb, in_=wsf), writes=[bw])
            done = 0
            while done < AG:
                n = min(8, AG - done)
                bi = 1 + (done // 8) % 2
                bkb = self.bank(bi).bitcast(BF16)
                for q in range(n):
                    g = done + q
                    S.op(PE, I("transpose", out=bkb[:, q * 128:(q + 1) * 128], in_=wsb[:, g * 128:(g + 1) * 128], identity=ident[:]),
                         reads=[bw, bc], writes=[self.b_ps[bi]])
                S.op(DVE, I("tensor_copy", out=self.wsT[:, done:done + n, :].rearrange("p g t -> p (g t)"), in_=bkb[:, 0:n * 128]),
                     writes=[self.b_ps[bi], self.b_wsT])
                done += n
            bsf = self.ms[0:1, 10240:10240 + AG * 128]
            bb = self.msb("setup_bs")
            S.op(POOL, I("dma_start", out=bsf, in_=self.w["a_b_s"][0:1].rearrange("o g t -> o (g t)")), writes=[bb], dma_slot=S.new_dma_slot())
            S.op(POOL, I("tensor_copy", out=self.bs_row[:], in_=bsf), reads=[bb], writes=[self.b_wsT])
        if has[1]:
            bp = self.msb("setup_pm")
            t0 = self.ms[:, 2048:2048 + 128]
            t1 = self.ms[:, 2304:2304 + 128]
            for wi, w in enumerate(c.WINS):
                iw = 1.0 / w
                S.op(POOL, I("memset", t0, iw), writes=[bp])
                S.op(POOL, I("affine_select", out=t0, in_=t0, pattern=[[1, 128]], compare_op=ALU.is_ge, fill=0.0, base=0, channel_multiplier=-1), writes=[bp])
                S.op(POOL, I("affine_select", out=t0, in_=t0, pattern=[[-1, 128]], compare_op=ALU.is_ge, fill=0.0, base=w - 1, channel_multiplier=1), writes=[bp])
                S.op(POOL, I("tensor_copy", out=t1, in_=t0), writes=[bp])
                for t in range(w - 1):
                    S.op(POOL, I("memset", t1[0:t + 1, t:t + 1], 1.0 / (t + 1)), writes=[bp])
                S.op(POOL, I("tensor_tensor", out=t1, in0=t1, in1=identf[:], op=ALU.subtract), reads=[bc], writes=[bp])
                S.op(POOL, I("tensor_copy", out=self.PM[:, wi, 2, :], in_=t1), writes=[bp, self.b_PM])
                S.op(POOL, I("tensor_tensor", out=t0, in0=t0, in1=identf[:], op=ALU.subtract), reads=[bc], writes=[bp])
                S.op(POOL, I("tensor_copy", out=self.PM[:, wi, 0, :], in_=t0), writes=[bp, self.b_PM])
                S.op(POOL, I("memset", t0, iw), writes=[bp])
                S.op(POOL, I("affine_select", out=t0, in_=t0, pattern=[[-1, 128]], compare_op=ALU.is_ge, fill=0.0, base=-(129 - w), channel_multiplier=1), writes=[bp])
                S.op(POOL, I("tensor_copy", out=self.PM[:, wi, 1, :], in_=t0), writes=[bp, self.b_PM])

    def setup_fence(self):
        self.ms_begin()

    def vcol(self, name, k, idx=0):
        r = self.rows[name] + idx
        return self.vecT[:, k, r:r + 1]

    def rsqrt(self, x_ap, x_bufs, out_ap, out_buf, n, cscale=1.0, eps=None, scratch=None):
        S = self.S
        eps = self.c.EPS if eps is None else eps
        if scratch is None:
            sc, sb = self.stat_cols(3 * n)
            xx, yy, tt = sc[:, 0:n], sc[:, n:2 * n], sc[:, 2 * n:3 * n]
        else:
            (xx, yy, tt), sb = scratch
        S.op(DVE, I("tensor_single_scalar", out=xx, in_=x_ap, scalar=eps, op=ALU.add), reads=x_bufs, writes=[sb])
        S.op(DVE, I("tensor_scalar", out=yy.bitcast(I32), in0=xx.bitcast(I32), scalar1=1, scalar2=None, op0=ALU.arith_shift_right), writes=[sb])
        S.op(DVE, I("tensor_scalar", out=yy.bitcast(I32), in0=yy.bitcast(I32), scalar1=-1, scalar2=0x5f3759df, op0=ALU.mult, op1=ALU.add), writes=[sb])
        for it in range(2):
            S.op(DVE, I("tensor_tensor", out=tt, in0=yy, in1=yy, op=ALU.mult), writes=[sb])
            S.op(DVE, I("tensor_tensor", out=tt, in0=tt, in1=xx, op=ALU.mult), writes=[sb])
            S.op(DVE, I("tensor_scalar", out=tt, in0=tt, scalar1=-0.5, scalar2=1.5, op0=ALU.mult, op1=ALU.add), writes=[sb])
            if it == 0:
                S.op(DVE, I("tensor_tensor", out=yy, in0=yy, in1=tt, op=ALU.mult), writes=[sb])
            else:
                S.op(DVE, I("scalar_tensor_tensor", out=out_ap, in0=yy, scalar=float(cscale), in1=tt, op0=ALU.mult, op1=ALU.mult),
                     reads=[sb], writes=[out_buf])

    def rsqrt1(self, x_ap, x_bufs, out_ap, out_buf):
        S = self.S
        sc, sb = self.stat_cols(4)
        xx, yy, hx, t = sc[:, 0:1], sc[:, 1:2], sc[:, 2:3], sc[:, 3:4]
        S.op(DVE, I("tensor_single_scalar", out=xx, in_=x_ap, scalar=self.c.EPS, op=ALU.add), reads=x_bufs, writes=[sb])
        S.op(DVE, I("tensor_scalar", out=yy.bitcast(I32), in0=xx.bitcast(I32), scalar1=1, scalar2=None, op0=ALU.arith_shift_right), writes=[sb])
        S.op(DVE, I("tensor_scalar", out=yy.bitcast(I32), in0=yy.bitcast(I32), scalar1=-1, scalar2=0x5f3759df, op0=ALU.mult, op1=ALU.add), writes=[sb])
        S.op(DVE, I("tensor_single_scalar", out=hx, in_=xx, scalar=-0.5, op=ALU.mult), writes=[sb])
        S.op(DVE, I("scalar_tensor_tensor", out=t, in0=yy, scalar=yy, in1=hx, op0=ALU.mult, op1=ALU.mult), writes=[sb])
        S.op(DVE, I("scalar_tensor_tensor", out=yy, in0=t, scalar=1.5, in1=yy, op0=ALU.add, op1=ALU.mult), writes=[sb])
        S.op(DVE, I("scalar_tensor_tensor", out=t, in0=yy, scalar=yy, in1=hx, op0=ALU.mult, op1=ALU.mult), writes=[sb])
        S.op(DVE, I("scalar_tensor_tensor", out=out_ap, in0=t, scalar=1.5, in1=yy, op0=ALU.add, op1=ALU.mult), reads=[sb], writes=[out_buf])

    def rstd(self, x_ap, x_bufs, out_ap, out_buf):
        if getattr(self, "rs_mode", "act") == "dve":
            return self.rsqrt1(x_ap, x_bufs, out_ap, out_buf)
        S = self.S
        sq, b_sq = self.stat_cols(1)
        S.op(ACT, I("activation", out=sq, in_=x_ap, func=AF.Sqrt, bias=self.eps_col[:, 0:1]), reads=list(x_bufs) + [self.b_const], writes=[b_sq])
        S.op(DVE, I("reciprocal", out=out_ap, in_=sq), reads=[b_sq], writes=[out_buf])

    def build_gt(self, gname, l):
        r = self.rows[gname] + l
        for k in range(self.c.KD):
            self.S.op(ACT, I("activation", out=self.gt[:, k, :], in_=self.vecT[:, k, r:r + 1].to_broadcast([128, 128]), func=AF.Copy),
                      reads=[self.b_vecT], writes=[self.b_gt], relax=(k > 0))

    def pre_a(self, tt):
        c, S = self.c, self.S
        msq, b_msq = self.stat_cols(1)
        rs, b_rs = self.stat_cols(1)
        S.op(ACT, I("activation", out=self.hs[:, tt + 1, :], in_=self.h[:, tt, :], func=AF.Square, scale=1.0 / math.sqrt(c.D), accum_out=msq),
             reads=[self.b_h[tt]], writes=[b_msq, self.b_hs[tt + 1]])
        self.rstd(msq, [b_msq], rs, b_rs)
        S.op(ACT, I("activation", out=self.hs[:, tt + 1, :], in_=self.h[:, tt, :], func=AF.Copy, scale=rs),
             reads=[self.b_h[tt], b_rs], writes=[self.b_hs[tt + 1]])

    def pre_b(self, tt):
        c, S = self.c, self.S
        KD = c.KD
        bi = tt % 2
        bkb = self.bank(bi).bitcast(BF16)
        for k in range(KD):
            S.op(PE, I("transpose", out=bkb[:, k * 128:(k + 1) * 128], in_=self.hs[:, tt + 1, k * 128:(k + 1) * 128], identity=self.ident[:]),
                 reads=[self.b_hs[tt + 1], self.b_const], writes=[self.b_ps[bi]])
        S.op(DVE, I("tensor_tensor", out=self.xnT[:, :, tt * 128:(tt + 1) * 128], in0=bkb[:, 0:KD * 128].rearrange("p (k t) -> p k t", k=KD), in1=self.gt[:], op=ALU.mult),
             reads=[self.b_gt], writes=[self.b_ps[bi], self.b_xnT[tt]])

    def prenorm_standalone(self, nxt):
        gname, l, transpose = nxt
        if transpose:
            self.build_gt(gname, l)
        for tt in range(self.c.NT):
            self.pre_a(tt)
            if transpose:
                self.pre_b(tt)

    def phase_begin(self, nxt, rs_mode="act"):
        self.rs_mode = rs_mode
        self.nxt = nxt
        self.pend_b = []
        if nxt is not None and nxt[2]:
            self.build_gt(nxt[0], nxt[1])

    def tile_pe_done(self, tt):
        keep = []
        for t in self.pend_b:
            if t < tt - 1:
                self.pre_b(t)
            else:
                keep.append(t)
        self.pend_b = keep

    def phase_end(self):
        for t in self.pend_b:
            self.pre_b(t)
        self.pend_b = []

    def load_gains(self, l):
        for i, n in enumerate(("ff1_post_g", "mix_post_g", "ff2_post_g", "ple_post_g")):
            src = self.w[n][l:l + 1, :].partition_broadcast(128)
            self.S.op(SP, I("dma_start", out=self.gb[:, i, :], in_=src), writes=[self.b_gb[i]], dma_slot=self.d_gb[i])

    def postnorm(self, tt, src_ap, src_bufs, src_is_psum, gslot, cc, mid=None):
        c, S = self.c, self.S
        D = c.D
        msq, b_msq = self.stat_cols(1)
        rs, b_rs = self.stat_cols(1)
        sr = [] if src_is_psum else list(src_bufs)
        sw = list(src_bufs) if src_is_psum else []
        S.op(ACT, I("activation", out=self.hs[:, tt + 1, :], in_=src_ap, func=AF.Square, scale=1.0 / math.sqrt(D), accum_out=msq),
             reads=sr, writes=sw + [b_msq, self.b_hs[tt + 1]])
        ti = self.rotate("tmp", 2)
        tmp = self.tmp[:, ti, :]
        S.op(DVE, I("scalar_tensor_tensor", out=tmp, in0=src_ap, scalar=float(cc), in1=self.gb[:, gslot, :], op0=ALU.mult, op1=ALU.mult),
             reads=sr + [self.b_gb[gslot]], writes=sw + [self.b_tmp[ti]])
        self.rstd(msq, [b_msq], rs, b_rs)
        S.op(DVE, I("scalar_tensor_tensor", out=self.h[:, tt, :], in0=tmp, scalar=rs, in1=self.h[:, tt, :], op0=ALU.mult, op1=ALU.add),
             reads=[self.b_tmp[ti], b_rs], writes=[self.b_h[tt]])
        if mid is not None:
            mid()
        if self.nxt is not None:
            self.pre_a(tt)
            if self.nxt[2]:
                self.pend_b.append(tt)

    def down_proj(self, nchunks, pieces, gslot, cc):
        c, S = self.c, self.S
        D = c.D
        per = pieces[0][0].shape[1]
        nh = D // 512
        for tt in range(c.NT):
            pair = (6, 4)[self.rotate("dn", 2)]
            for hf in range(nh):
                bi = pair + hf
                for ch in range(nchunks):
                    wap, wbuf = pieces[ch // per]
                    S.op(PE, I("matmul", self.bank(bi), lhsT=self.actT[:, ch, tt * 128:(tt + 1) * 128],
                                                                                    rhs=wap[:, ch % per, hf * 512:(hf + 1) * 512], start=(ch == 0), stop=(ch == nchunks - 1)),
                         reads=[self.b_act[ch], wbuf], writes=[self.b_ps[bi]])
            src = self.bank2(pair) if nh == 2 else self.bank(pair)
            bufs = [self.b_ps[pair + i] for i in range(nh)]
            self.postnorm(tt, src, bufs, True, gslot, cc, mid=lambda tt=tt: self.tile_pe_done(tt))
        self.phase_end()

    def down_pieces(self, name, li, nchunks):
        sc, sbuf = self.scratch_up(name, li, nchunks * 128, self.c.D)
        per = 2048 // self.c.D
        out = []
        for i in range(0, nchunks, per):
            n = min(per, nchunks - i)
            out.append(self.wload(sc[:, i:i + n, :], [n, self.c.D], sbuf))
        return out, per

    def ffn(self, l, which, nxt):
        c, S = self.c, self.S
        D, KD, KF = c.D, c.KD, c.KF
        pre = "ff%d" % which
        self.phase_begin(nxt)
        scg, bg = self.scratch_up(pre + "_w_gate", l, D, c.F)
        scu, bu = self.scratch_up(pre + "_w_up", l, D, c.F)
        xb = self.b_xnT
        KH = KD // 2

        def pair(j, gsel, usel, q, bG, bU):
            for k in range(KD):
                wap, wbuf, kk = gsel(k)
                S.op(PE, I("matmul", self.bank(bG), lhsT=wap[:, kk, q * 128:(q + 1) * 128], rhs=self.xnT[:, k, :], start=(k == 0), stop=(k == KD - 1)),
                     reads=xb + [wbuf], writes=[self.b_ps[bG]])
            for k in range(KD):
                wap, wbuf, kk = usel(k)
                S.op(PE, I("matmul", self.bank(bU), lhsT=wap[:, kk, q * 128:(q + 1) * 128], rhs=self.xnT[:, k, :], start=(k == 0), stop=(k == KD - 1)),
                     reads=xb + [wbuf], writes=[self.b_ps[bU]])
            si = self.rotate("sil", 2)
            S.op(ACT, I("activation", out=self.sil[:, si, :], in_=self.bank(bG), func=AF.Silu), writes=[self.b_ps[bG], self.b_sil[si]])
            S.op(DVE, I("tensor_tensor", out=self.actT[:, j, :], in0=self.sil[:, si, :], in1=self.bank(bU), op=ALU.mult),
                 reads=[self.b_sil[si]], writes=[self.b_ps[bU], self.b_act[j]])

        ngrp = c.F // 512
        for g in range(ngrp):
            c0 = g * 512
            wg = [self.wload(scg[:, kh * KH:(kh + 1) * KH, c0:c0 + 512], [KH, 512], bg) for kh in range(2)]
            wu = [self.wload(scu[:, kh * KH:(kh + 1) * KH, c0:c0 + 512], [KH, 512], bu) for kh in range(2)]
            for q in range(4):
                bG, bU = (2, 3) if q % 2 == 0 else (4, 5)
                pair(g * 4 + q, lambda k: (wg[k // KH][0], wg[k // KH][1], k % KH), lambda k: (wu[k // KH][0], wu[k // KH][1], k % KH), q, bG, bU)
        rem = (c.F - ngrp * 512) // 128
        if rem:
            c0 = ngrp * 512
            wgr = self.wload(scg[:, :, c0:c0 + rem * 128], [KD, rem * 128], bg)
            wur = self.wload(scu[:, :, c0:c0 + rem * 128], [KD, rem * 128], bu)
            for q in range(rem):
                bG, bU = (2, 3) if q % 2 == 0 else (4, 5)
                pair(ngrp * 4 + q, lambda k: (wgr[0], wgr[1], k), lambda k: (wur[0], wur[1], k), q, bG, bU)
        pieces, per = self.down_pieces(pre + "_w_down", l, KF)
        self.down_proj(KF, pieces, 0 if which == 1 else 2, 0.5)

    def up_single(self, sc, sbuf, col0, ncols_chunks, evac):
        c, S = self.c, self.S
        KD = c.KD
        KH = KD // 2
        jj = 0
        while jj < ncols_chunks:
            n = min(4, ncols_chunks - jj)
            if n == 4:
                ws = [self.wload(sc[:, kh * KH:(kh + 1) * KH, col0 + jj * 128:col0 + (jj + 4) * 128], [KH, 512], sbuf) for kh in range(2)]
                sel = lambda k, ws=ws: (ws[k // KH][0], ws[k // KH][1], k % KH)
            else:
                n = min(2, n)
                wr = self.wload(sc[:, :, col0 + jj * 128:col0 + (jj + n) * 128], [KD, n * 128], sbuf)
                sel = lambda k, wr=wr: (wr[0], wr[1], k)
            for q in range(n):
                bi = 2 + self.rotate("up1", 4)
                for k in range(KD):
                    wap, wbuf, kk = sel(k)
                    S.op(PE, I("matmul", self.bank(bi), lhsT=wap[:, kk, q * 128:(q + 1) * 128], rhs=self.xnT[:, k, :], start=(k == 0), stop=(k == KD - 1)),
                         reads=self.b_xnT + [wbuf], writes=[self.b_ps[bi]])
                evac(jj + q, bi)
            jj += n

    def grouped_pieces(self, sc, sb, colA, colB):
        KD = self.c.KD
        KH = KD // 2
        cache = {}

        def sel(j, k):
            g = j // 4
            if g not in cache:
                cache[g] = ([self.wload(sc[:, kh * KH:(kh + 1) * KH, colA + g * 512:colA + (g + 1) * 512], [KH, 512], sb) for kh in range(2)],
                            [self.wload(sc[:, kh * KH:(kh + 1) * KH, colB + g * 512:colB + (g + 1) * 512], [KH, 512], sb) for kh in range(2)])
            pa, pb = cache[g]
            return pa[k // KH], pb[k // KH], k % KH, (j % 4) * 128
        return sel

    def mixer_gmlp(self, l, blk, nxt):
        c, S = self.c, self.S
        self.ms_begin()
        self.phase_begin(nxt)
        D, KD, KA, AH, AG = c.D, c.KD, c.KA, c.AH, c.AG
        j0 = l // 4
        sc, sb = self.scratch_up("a_w_in", j0, D, 2 * AH)
        vg = self.ms[:, 0:AH]
        vb = self.ms[:, AH:2 * AH]
        vt = self.ms[:, 2 * AH:3 * AH]
        vn = [self.ms[:, 3 * AH + i * (AH // 2):3 * AH + (i + 1) * (AH // 2)].bitcast(BF16) for i in range(2)]
        assert (3 * AH + AH) * 4 <= self.MSB
        b_vgb = self.msb("vgb"); b_vt = self.msb("vt"); b_vn = [self.msb("vn0"), self.msb("vn1")]
        vt = [vt, vt]
        b_vt = [b_vt, b_vt]
        if "d_vgb" not in self.__dict__:
            self.d_vgb = self.S.new_dma_slot()
        S.op(SP, I("dma_start", out=vg, in_=self.w["a_v_norm_g"][j0:j0 + 1, :].partition_broadcast(128)), writes=[b_vgb], dma_slot=self.d_vgb)
        S.op(SP, I("dma_start", out=vb, in_=self.w["a_v_norm_b"][j0:j0 + 1, :].partition_broadcast(128)), writes=[b_vgb], dma_slot=self.d_vgb)
        def evac_u(j, bi):
            S.op(ACT, I("activation", out=self.actT[:, j, :], in_=self.bank(bi), func=AF.Gelu_apprx_tanh), writes=[self.b_ps[bi], self.b_act[j]])
        self.up_single(sc, sb, 0, KA, evac_u)
        ncg = AH // 512
        KH = KD // 2
        vp = []
        for cg in range(ncg):
            for kh in range(2):
                vp.append(self.wload(sc[:, kh * KH:(kh + 1) * KH, AH + cg * 512:AH + (cg + 1) * 512], [KH, 512], sb))
        state = {}

        def vpath(tt):
            vi = self.rotate("vn", 2)
            sm, b_sm = self.stat_cols(ncg)
            sq, b_sq = self.stat_cols(ncg)
            for cg in range(ncg):
                bi = 2 + self.rotate("up1", 4)
                for k in range(KD):
                    wap, wbuf = vp[cg * 2 + k // KH]
                    S.op(PE, I("matmul", self.bank(bi), lhsT=self.xnT[:, k, tt * 128:(tt + 1) * 128], rhs=wap[:, k % KH, :], start=(k == 0), stop=(k == KD - 1)),
                         reads=[self.b_xnT[tt], wbuf], writes=[self.b_ps[bi]])
                S.op(ACT, I("activation", out=vt[tt % 2][:, cg * 512:(cg + 1) * 512], in_=self.bank(bi), func=AF.Gelu_apprx_tanh, accum_out=sm[:, cg:cg + 1]),
                     writes=[self.b_ps[bi], b_vt[tt % 2], b_sm])
                S.op(ACT, I("activation", out=vn[vi][:, cg * 512:(cg + 1) * 512], in_=vt[tt % 2][:, cg * 512:(cg + 1) * 512], func=AF.Square, accum_out=sq[:, cg:cg + 1]),
                     reads=[b_vt[tt % 2]], writes=[b_sq, b_vn[vi]])
            st, b_st = self.stat_cols(6)
            mean, ex2, var = st[:, 0:1], st[:, 1:2], st[:, 2:3]
            S.op(DVE, I("tensor_reduce", out=mean, in_=sm, axis=mybir.AxisListType.X, op=ALU.add), reads=[b_sm], writes=[b_st])
            S.op(DVE, I("tensor_reduce", out=ex2, in_=sq, axis=mybir.AxisListType.X, op=ALU.add), reads=[b_sq], writes=[b_st])
            S.op(DVE, I("tensor_scalar", out=st[:, 0:2], in0=st[:, 0:2], scalar1=1.0 / AH, scalar2=None, op0=ALU.mult), writes=[b_st])
            S.op(DVE, I("tensor_tensor", out=var, in0=mean, in1=mean, op=ALU.mult), writes=[b_st])
            S.op(DVE, I("tensor_tensor", out=var, in0=ex2, in1=var, op=ALU.subtract), writes=[b_st])
            S.op(DVE, I("tensor_scalar", out=var, in0=var, scalar1=0.0, scalar2=None, op0=ALU.max), writes=[b_st])
            rs, b_rs = self.stat_cols(1)
            self.rsqrt1(var, [b_st], rs, b_rs)
            v = vt[tt % 2]
            S.op(DVE, I("tensor_scalar", out=v, in0=v, scalar1=mean, scalar2=rs, op0=ALU.subtract, op1=ALU.mult), reads=[b_st, b_rs], writes=[b_vt[tt % 2]])
            S.op(DVE, I("tensor_tensor", out=v, in0=v, in1=vg, op=ALU.mult), reads=[b_vgb], writes=[b_vt[tt % 2]])
            S.op(DVE, I("tensor_tensor", out=vn[vi], in0=v, in1=vb, op=ALU.add), reads=[b_vgb, b_vt[tt % 2]], writes=[b_vn[vi]])
            state[tt] = vi

        def spatial(tt):
            vi = state[tt]
            for c4 in range(KA // 4):
                bi = 2 + self.rotate("up1", 4)
                for q in range(4):
                    ch = c4 * 4 + q
                    g = ch // 2
                    S.op(PE, I("matmul", self.bank(bi)[:, q * 128:(q + 1) * 128], lhsT=self.ones_row[0:1, :], rhs=self.bs_row[0:1, g * 128:(g + 1) * 128], start=True, stop=False),
                         reads=[self.b_const, self.b_wsT], writes=[self.b_ps[bi]])
                    S.op(PE, I("matmul", self.bank(bi)[:, q * 128:(q + 1) * 128], lhsT=vn[vi][:, ch * 128:(ch + 1) * 128], rhs=self.wsT[:, g, :], start=False, stop=True),
                         reads=[b_vn[vi], self.b_wsT], writes=[self.b_ps[bi]])
                ya = self.actT[:, c4 * 4:(c4 + 1) * 4, tt * 128:(tt + 1) * 128]
                S.op(DVE, I("tensor_tensor", out=ya, in0=ya, in1=self.bank(bi).rearrange("p (q t) -> p q t", q=4), op=ALU.mult),
                     writes=[self.b_ps[bi]] + [self.b_act[c4 * 4 + q] for q in range(4)])

        vpath(0)
        for tt in range(1, c.NT):
            vpath(tt)
            spatial(tt - 1)
        spatial(c.NT - 1)
        pieces, per = self.down_pieces("a_w_out", j0, KA)
        self.down_proj(KA, pieces, 1, 1.0)

    def mixer_pool(self, l, blk, nxt):
        c, S = self.c, self.S
        self.ms_begin()
        self.phase_begin(nxt)
        D, KD = c.D, c.KD
        j0 = l // 4
        kc = c.PG // 128
        scg, bg = self.scratch_grp(j0)
        wg, wgb = self.wload(scg.rearrange("p g k d -> p (g k) d"), [4 * kc, c.PG], bg)
        bsc = self.ms[:, 0:D]
        b_bsc = self.msb("bscale")
        if "d_bsc" not in self.__dict__:
            self.d_bsc = S.new_dma_slot()
        S.op(SP, I("dma_start", out=bsc, in_=self.w["b_scale"][j0:j0 + 1, :].partition_broadcast(128)), writes=[b_bsc], dma_slot=self.d_bsc)
        for ch in range(KD):
            wi = ch // kc
            bi = 2 + self.rotate("up1", 4)
            for tt in range(c.NT):
                first = (blk == 0 and tt == 0)
                o = self.bank(bi)[:, tt * 128:(tt + 1) * 128]
                S.op(PE, I("matmul", o, lhsT=self.hs[:, tt + 1, ch * 128:(ch + 1) * 128], rhs=self.PM[:, wi, 2 if first else 0, :], start=True, stop=first),
                     reads=[self.b_hs[tt + 1], self.b_PM], writes=[self.b_ps[bi]])
                if not first:
                    S.op(PE, I("matmul", o, lhsT=self.hs[:, tt, ch * 128:(ch + 1) * 128], rhs=self.PM[:, wi, 1, :], start=False, stop=True),
                         reads=[self.b_hs[tt], self.b_PM], writes=[self.b_ps[bi]])
            S.op(ACT, I("activation", out=self.actT[:, ch, :], in_=self.bank(bi), func=AF.Copy, scale=self.vecT[:, ch, self.rows["mix_pre_g"] + l:self.rows["mix_pre_g"] + l + 1]),
                 reads=[self.b_vecT], writes=[self.b_ps[bi], self.b_act[ch]])
        S.op(ACT, I("activation", out=self.hs[:, 0, :], in_=self.hs[:, c.NT, :], func=AF.Copy), reads=[self.b_hs[c.NT]], writes=[self.b_hs[0]])
        nh = D // 512
        for tt in range(c.NT):
            pair = (6, 4)[self.rotate("dn", 2)]
            for g in range(4):
                col = g * c.PG
                bi = pair + col // 512
                for k in range(kc):
                    S.op(PE, I("matmul", self.bank(bi)[:, col % 512:col % 512 + c.PG], lhsT=self.actT[:, g * kc + k, tt * 128:(tt + 1) * 128],
                                                                                rhs=wg[:, g * kc + k, :], start=(k == 0), stop=(k == kc - 1)),
                         reads=[self.b_act[g * kc + k], wgb], writes=[self.b_ps[bi]])
            src = self.bank2(pair) if nh == 2 else self.bank(pair)
            bufs = [self.b_ps[pair + i] for i in range(nh)]
            yi = self.rotate("ysc", 2)
            ysc = self.ms[:, D * (1 + yi):D * (2 + yi)]
            b_y = self.msb("ysc%d" % yi)
            S.op(DVE, I("tensor_tensor", out=ysc, in0=src, in1=bsc, op=ALU.mult), reads=[b_bsc], writes=bufs + [b_y])
            self.postnorm(tt, ysc, [b_y], False, 1, 1.0, mid=lambda tt=tt: self.tile_pe_done(tt))
        self.phase_end()

    def conv_taps(self, eng, zbuf_ap, z_bufs, acc_ap, acc_buf, ch, ntap, wname, bias_name=None):
        S = self.S
        TB = self.c.TB
        for k in range(ntap):
            wk = self.vcol(wname, ch, k)
            if k == 0:
                if bias_name is not None:
                    S.op(eng, I("tensor_scalar", out=acc_ap, in0=zbuf_ap[:, 0:TB], scalar1=wk, scalar2=self.vcol(bias_name, ch), op0=ALU.mult, op1=ALU.add),
                         reads=z_bufs + [self.b_vecT], writes=[acc_buf])
                else:
                    S.op(eng, I("tensor_scalar", out=acc_ap, in0=zbuf_ap[:, 0:TB], scalar1=wk, scalar2=None, op0=ALU.mult),
                         reads=z_bufs + [self.b_vecT], writes=[acc_buf])
            else:
                S.op(eng, I("scalar_tensor_tensor", out=acc_ap, in0=zbuf_ap[:, k:k + TB], scalar=wk, in1=acc_ap, op0=ALU.mult, op1=ALU.add),
                     reads=z_bufs + [self.b_vecT], writes=[acc_buf])

    def scratch_diag(self):
        if "diag" in self.scr:
            return self.scr["diag"]
        c, S = self.c, self.S
        KD, CW = c.KD, c.CW
        sc = self.nc.dram_tensor("sc_diag", [128, KD, CW, 128], BF16, kind="Internal").ap()
        buf = Buf("sc_diag")
        slot = S.new_dma_slot()
        r0 = self.rows["c_w_dw"]
        half = max(1, KD // 2)
        nw = half * CW * 64
        assert nw <= self.MSB // 4
        dg = self.ms[:, 0:nw].bitcast(BF16).rearrange("p (j k m) -> p j k m", j=half, k=CW)
        b_dg = self.msb("dg")
        for j0 in range(0, KD, half):
            for jj in range(half):
                j = j0 + jj
                for k in range(CW):
                    S.op(DVE, I("tensor_scalar", out=dg[:, jj, k, :], in0=self.ident[:], scalar1=self.vecT[:, j, r0 + k:r0 + k + 1], scalar2=None, op0=ALU.mult),
                         reads=[self.b_vecT, self.b_const], writes=[b_dg], relax=(jj + k > 0))
            last = (j0 + half >= KD)
            S.op(POOL, I("dma_start", out=sc[:, j0:j0 + half, :, :], in_=dg), reads=[b_dg], writes=[buf] if last else (), dma_slot=slot)
        self.scr["diag"] = (sc, buf)
        return self.scr["diag"]

    def mixer_conformer(self, l, blk, nxt):
        c, S = self.c, self.S
        self.ms_begin()
        self.phase_begin(nxt)
        D, KD, TB, CW = c.D, c.KD, c.TB, c.CW
        H = CW - 1
        j0 = l // 4
        sc, sb = self.scratch_up("c_w_pw1", j0, D, 2 * D)
        scd, sbd = self.scratch_diag()
        ZW = H + TB
        nz = (KD * ZW + 1) // 2
        z = self.ms[:, 0:nz].bitcast(BF16)[:, 0:KD * ZW].rearrange("p (k w) -> p k w", k=KD)
        o1 = ((nz + 7) // 8) * 8
        acc = self.ms[:, o1:o1 + KD * TB].rearrange("p (k w) -> p k w", k=KD)
        o2 = o1 + KD * TB
        sqt = [self.ms[:, o2 + i * TB:o2 + (i + 1) * TB] for i in range(2)]
        o3 = o2 + 2 * TB
        mean_sb = self.ms[:, o3:o3 + TB]
        rstd_sb = self.ms[:, o3 + TB:o3 + 2 * TB]
        nsc = [self.ms[:, o3 + (2 + i) * TB:o3 + (3 + i) * TB] for i in range(3)]
        assert (o3 + 5 * TB) * 4 <= self.MSB
        b_z = [self.msb("z%d" % k) for k in range(KD)]
        b_acc = [self.msb("acc%d" % k) for k in range(KD)]
        b_sq = [self.msb("sq0"), self.msb("sq1")]
        b_mr = self.msb("meanrstd")
        b_nsc = self.msb("nsc")
        for k in range(KD):
            S.op(ACT, I("activation", out=z[:, k, 0:H], in_=self.halo_c[:, k, :], func=AF.Copy), reads=[self.b_halo_c], writes=[b_z[k]])

        sel_pw1 = self.grouped_pieces(sc, sb, 0, D)

        def pw1(j):
            bA, bG = (2, 3) if j % 2 == 0 else (4, 5)
            for k in range(KD):
                (wp, wb), _, kk, co = sel_pw1(j, k)
                S.op(PE, I("matmul", self.bank(bA), lhsT=wp[:, kk, co:co + 128], rhs=self.xnT[:, k, :], start=(k == 0), stop=(k == KD - 1)),
                     reads=self.b_xnT + [wb], writes=[self.b_ps[bA]])
            for k in range(KD):
                _, (wq, wqb), kk, co = sel_pw1(j, k)
                S.op(PE, I("matmul", self.bank(bG), lhsT=wq[:, kk, co:co + 128], rhs=self.xnT[:, k, :], start=(k == 0), stop=(k == KD - 1)),
                     reads=self.b_xnT + [wqb], writes=[self.b_ps[bG]])
            si = self.rotate("sil", 2)
            S.op(ACT, I("activation", out=self.sil[:, si, :], in_=self.bank(bG), func=AF.Sigmoid), writes=[self.b_ps[bG], self.b_sil[si]])
            S.op(DVE, I("tensor_tensor", out=z[:, j, H:H + TB], in0=self.sil[:, si, :], in1=self.bank(bA), op=ALU.mult),
                 reads=[self.b_sil[si]], writes=[self.b_ps[bA], b_z[j]])
            S.op(ACT, I("activation", out=self.halo_c[:, j, :], in_=z[:, j, TB:TB + H], func=AF.Copy), reads=[b_z[j]], writes=[self.b_halo_c])

        def conv(j):
            n0 = min(16, CW)
            p0 = self.wload(scd[:, j, 0:n0, :], [n0, 128], sbd)
            p1 = self.wload(scd[:, j, n0:CW, :], [CW - n0, 128], sbd) if CW > n0 else None
            bi = 6 + (j % 2)
            for k in range(CW):
                wap, wbuf = p0 if k < n0 else p1
                S.op(PE, I("matmul", self.bank(bi), lhsT=wap[:, k % n0 if k < n0 else k - n0, :], rhs=z[:, j, k:k + TB], start=(k == 0), stop=(k == CW - 1)),
                     reads=[b_z[j], wbuf], writes=[self.b_ps[bi]])
            S.op(ACT, I("activation", out=acc[:, j, :], in_=self.bank(bi), func=AF.Identity, bias=self.vcol("c_b_dw", j)),
                 reads=[self.b_vecT], writes=[self.b_ps[bi], b_acc[j]])
            qi = j % 2
            S.op(ACT, I("activation", out=sqt[qi], in_=acc[:, j, :], func=AF.Square), reads=[b_acc[j]], writes=[b_sq[qi]])

        def stats(j):
            qi = j % 2
            S.op(PE, I("matmul", self.bank(0), lhsT=self.onesD[:], rhs=acc[:, j, :], start=(j == 0), stop=(j == KD - 1)),
                 reads=[b_acc[j], self.b_const], writes=[self.b_ps[0]])
            S.op(PE, I("matmul", self.bank(1), lhsT=self.onesD[:], rhs=sqt[qi], start=(j == 0), stop=(j == KD - 1)),
                 reads=[b_sq[qi], self.b_const], writes=[self.b_ps[1]])

        for j in range(KD):
            pw1(j)
            if j >= 1:
                conv(j - 1)
            if j >= 2:
                stats(j - 2)
        conv(KD - 1)
        if KD >= 2:
            stats(KD - 2)
        stats(KD - 1)
        S.op(DVE, I("tensor_copy", out=mean_sb, in_=self.bank(0)), writes=[self.b_ps[0], b_mr])
        S.op(DVE, I("tensor_tensor", out=nsc[0], in0=mean_sb, in1=mean_sb, op=ALU.mult), reads=[b_mr], writes=[b_nsc])
        S.op(DVE, I("tensor_tensor", out=nsc[0], in0=self.bank(1), in1=nsc[0], op=ALU.subtract), writes=[self.b_ps[1], b_nsc])
        S.op(DVE, I("tensor_scalar", out=nsc[0], in0=nsc[0], scalar1=0.0, scalar2=None, op0=ALU.max), writes=[b_nsc])
        S.op(ACT, I("activation", out=nsc[1], in_=nsc[0], func=AF.Sqrt, bias=self.eps_col[:, 0:1]), reads=[self.b_const], writes=[b_nsc])
        S.op(DVE, I("reciprocal", out=rstd_sb, in_=nsc[1]), reads=[b_nsc], writes=[b_mr])
        for j in range(KD):
            eng = POOL if j % 3 == 2 else DVE
            S.op(eng, I("tensor_tensor", out=acc[:, j, :], in0=acc[:, j, :], in1=mean_sb, op=ALU.subtract), reads=[b_mr], writes=[b_acc[j]])
            S.op(eng, I("tensor_tensor", out=acc[:, j, :], in0=acc[:, j, :], in1=rstd_sb, op=ALU.mult), reads=[b_mr], writes=[b_acc[j]])
            S.op(ACT, I("activation", out=self.actT[:, j, :], in_=acc[:, j, :], func=AF.Silu, scale=self.vcol("c_norm_g", j), bias=self.vcol("c_norm_b", j)),
                 reads=[b_acc[j], self.b_vecT], writes=[self.b_act[j]])
        pieces, per = self.down_pieces("c_w_pw2", j0, KD)
        self.down_proj(KD, pieces, 1, 1.0)

    def mixer_shortconv(self, l, blk, nxt):
        c, S = self.c, self.S
        self.ms_begin()
        self.phase_begin(nxt)
        D, KD, TB, SW = c.D, c.KD, c.TB, c.SW
        H = SW - 1
        j0 = l // 4
        sc, sb = self.scratch_up("d_w_in", j0, D, 3 * D)
        ZW = H + TB
        m = self.ms[:, 0:KD * ZW].rearrange("p (k w) -> p k w", k=KD)
        acc = self.ms[:, KD * ZW:KD * ZW + KD * TB].rearrange("p (k w) -> p k w", k=KD)
        assert (KD * ZW + KD * TB) * 4 <= self.MSB
        b_m = [self.msb("m%d" % k) for k in range(KD)]
        b_acc = [self.msb("sacc%d" % k) for k in range(KD)]
        for k in range(KD):
            S.op(ACT, I("activation", out=m[:, k, 0:H], in_=self.halo_d[:, k, :], func=AF.Copy), reads=[self.b_halo_d], writes=[b_m[k]])
        sel_in = self.grouped_pieces(sc, sb, D, 2 * D)
        for j in range(KD):
            bC, bX = (2, 3) if j % 2 == 0 else (4, 5)
            for k in range(KD):
                (wp, wb), _, kk, co = sel_in(j, k)
                S.op(PE, I("matmul", self.bank(bC), lhsT=wp[:, kk, co:co + 128], rhs=self.xnT[:, k, :], start=(k == 0), stop=(k == KD - 1)),
                     reads=self.b_xnT + [wb], writes=[self.b_ps[bC]])
            for k in range(KD):
                _, (wq, wqb), kk, co = sel_in(j, k)
                S.op(PE, I("matmul", self.bank(bX), lhsT=wq[:, kk, co:co + 128], rhs=self.xnT[:, k, :], start=(k == 0), stop=(k == KD - 1)),
                     reads=self.b_xnT + [wqb], writes=[self.b_ps[bX]])
            si = self.rotate("sil", 2)
            S.op(ACT, I("activation", out=self.sil[:, si, :], in_=self.bank(bC), func=AF.Copy), writes=[self.b_ps[bC], self.b_sil[si]])
            S.op(DVE, I("tensor_tensor", out=m[:, j, H:H + TB], in0=self.sil[:, si, :], in1=self.bank(bX), op=ALU.mult),
                 reads=[self.b_sil[si]], writes=[self.b_ps[bX], b_m[j]])
            eng = DVE
            self.conv_taps(eng, m[:, j, :], [b_m[j]], acc[:, j, :], b_acc[j], j, SW, "d_w_conv")
            S.op(ACT, I("activation", out=self.halo_d[:, j, :], in_=m[:, j, TB:TB + H], func=AF.Copy), reads=[b_m[j]], writes=[self.b_halo_d])

        def evac_bg(j, bi):
            S.op(DVE, I("tensor_tensor", out=self.actT[:, j, :], in0=acc[:, j, :], in1=self.bank(bi), op=ALU.mult),
                 reads=[b_acc[j]], writes=[self.b_ps[bi], self.b_act[j]])
        self.up_single(sc, sb, 0, KD, evac_bg)
        pieces, per = self.down_pieces("d_w_out", j0, KD)
        self.down_proj(KD, pieces, 1, 1.0)

    def ple(self, l, blk, nxt):
        c, S = self.c, self.S
        self.ms_begin()
        self.phase_begin(nxt, rs_mode="act")
        D, KD, KP, TB = c.D, c.KD, c.KP, c.TB
        o0 = 2 * D
        n1 = c.NT * c.PLE // 2
        n2 = KP * TB // 2
        p_sb = self.ms[:, o0:o0 + n1].bitcast(BF16).rearrange("p (t c) -> p t c", t=c.NT)
        pT = self.ms[:, o0 + n1:o0 + n1 + n2].bitcast(BF16).rearrange("p (k t) -> p k t", k=KP)
        b_p_sb = self.msb("p_sb")
        b_pT = self.msb("pT")
        S.op(POOL, I("dma_start", out=p_sb, in_=self.p[l, blk * TB:(blk + 1) * TB, :].rearrange("(t q) c -> q t c", q=128)),
             writes=[b_p_sb], dma_slot=self.d_p)
        bi = 1
        bkb = self.bank(bi).bitcast(BF16)
        for tt in range(c.NT):
            for k in range(KP):
                S.op(PE, I("transpose", out=bkb[:, k * TB + tt * 128:k * TB + (tt + 1) * 128], in_=p_sb[:, tt, k * 128:(k + 1) * 128], identity=self.ident[:]),
                     reads=[b_p_sb, self.b_const], writes=[self.b_ps[bi]])
        S.op(DVE, I("tensor_copy", out=pT.rearrange("p k t -> p (k t)"), in_=bkb[:, 0:KP * TB]), writes=[self.b_ps[bi], b_pT])
        scg, bg = self.scratch_up("ple_w_gate", l, D, D)
        scj, bj = self.scratch_up("ple_w_proj", l, c.PLE, D)
        nh = D // 512
        gp = {}
        KH = KD // 2
        for hf in range(nh):
            for kh in range(2):
                gp[(hf, kh)] = self.wload(scg[:, kh * KH:(kh + 1) * KH, hf * 512:(hf + 1) * 512], [KH, 512], bg)
        pj = [self.wload(scj[:, :, hf * 512:(hf + 1) * 512], [KP, 512], bj) for hf in range(nh)]
        ets = []
        for tt in range(c.NT):
            oe = 2 * D + 1024 + tt * D
            et = self.ms[:, oe:oe + D]
            b_e = self.msb("etile%d" % tt)
            ets.append((et, b_e))
            for hf in range(nh):
                bG, bP = (2, 3) if (tt * nh + hf) % 2 == 0 else (4, 5)
                for k in range(KD):
                    wap, wbuf = gp[(hf, k // KH)]
                    S.op(PE, I("matmul", self.bank(bG), lhsT=self.xnT[:, k, tt * 128:(tt + 1) * 128], rhs=wap[:, k % KH, :], start=(k == 0), stop=(k == KD - 1)),
                         reads=[self.b_xnT[tt], wbuf], writes=[self.b_ps[bG]])
                wap, wbuf = pj[hf]
                for k in range(KP):
                    S.op(PE, I("matmul", self.bank(bP), lhsT=pT[:, k, tt * 128:(tt + 1) * 128], rhs=wap[:, k, :], start=(k == 0), stop=(k == KP - 1)),
                         reads=[b_pT, wbuf], writes=[self.b_ps[bP]])
                si = self.rotate("sil", 2)
                S.op(ACT, I("activation", out=self.sil[:, si, :], in_=self.bank(bG), func=AF.Sigmoid), writes=[self.b_ps[bG], self.b_sil[si]])
                S.op(DVE, I("tensor_tensor", out=et[:, hf * 512:(hf + 1) * 512], in0=self.sil[:, si, :], in1=self.bank(bP), op=ALU.mult),
                     reads=[self.b_sil[si]], writes=[self.b_ps[bP], b_e])
        for tt in range(c.NT):
            et, b_e = ets[tt]
            self.postnorm(tt, et, [b_e], False, 3, 1.0)
        self.phase_end()

    def build(self):
        c, S = self.c, self.S
        L = self.layers
        self.setup()
        for blk in range(c.NBLK):
            self.blk = blk
            for tt in range(c.NT):
                r0 = (blk * c.NT + tt) * 128
                S.op(SP, I("dma_start", out=self.h[:, tt, :], in_=self.x[r0:r0 + 128, :]), writes=[self.b_h[tt]], dma_slot=self.d_x[tt])
            if blk == 0:
                self.load_gains(L[0])
                if 2 in [l % 4 for l in L]:
                    self.scratch_diag()
                    self.setup_fence()
                self.setup_b()
                self.setup_fence()
            self.prenorm_standalone(("ff1_pre_g", L[0], True))
            for i, l in enumerate(L):
                if not (blk == 0 and i == 0):
                    self.load_gains(l)
                m = l % 4
                self.mark("b%d_l%d_ffn1" % (blk, l))
                self.ffn(l, 1, ("mix_pre_g", l, m != 1))
                self.mark("b%d_l%d_mix" % (blk, l))
                nx = ("ff2_pre_g", l, True)
                if m == 0:
                    self.mixer_gmlp(l, blk, nx)
                elif m == 1:
                    self.mixer_pool(l, blk, nx)
                elif m == 2:
                    self.mixer_conformer(l, blk, nx)
                else:
                    self.mixer_shortconv(l, blk, nx)
                self.mark("b%d_l%d_ffn2" % (blk, l))
                self.ffn(l, 2, ("ple_gate_norm_g", l, True))
                self.mark("b%d_l%d_ple" % (blk, l))
                self.ple(l, blk, ("ff1_pre_g", L[i + 1], True) if i + 1 < len(L) else None)
            for tt in range(c.NT):
                r0 = (blk * c.NT + tt) * 128
                o = S.op(SP, I("dma_start", out=self.y[r0:r0 + 128, :], in_=self.h[:, tt, :]), reads=[self.b_h[tt]], dma_slot=self.d_y[tt])
                self.final_ops.append(o)
        S.finalize_and_emit(final_waits=self.final_ops[-c.NT:])
        return self.nc


LAUNCH_GROUPS = [[0, 1, 2, 3]]
_NC_CACHE = {}


def _get_nc(cfg, layers):
    key = (id(cfg), tuple(layers))
    if key not in _NC_CACHE:
        _NC_CACHE[key] = Builder(cfg, layers).build()
    return _NC_CACHE[key]


def kernel(**inputs):
    c = FULL
    n_cores = 8
    x = np.ascontiguousarray(inputs["x"], dtype=np.float32)
    p = inputs["p"]
    wts = {n: np.ascontiguousarray(inputs[n], dtype=np.float32) for n in WEIGHT_NAMES}
    p_sh = [np.ascontiguousarray(p[:, b]) for b in range(n_cores)]
    cur = [x[b] for b in range(n_cores)]
    for layers in LAUNCH_GROUPS:
        nc = _get_nc(c, layers)
        in_maps = []
        for b in range(n_cores):
            m = {"x": cur[b], "p": p_sh[b]}
            m.update(wts)
            in_maps.append(m)
        res = run_bass_kernel_spmd(nc, in_maps, core_ids=list(range(n_cores)))
        cur = [np.asarray(r["y"]) for r in res.results]
    return np.stack(cur, axis=0).astype(np.float32)
```
